# Optimizing a Trainium2 kernel written in Bass

```python
import math
import jax, jax.numpy as jnp
from jax import lax
import numpy as np

D_MODEL = 2048
BATCH = 4
SEQ = 2048
DEPTH = 4
DEC_BATCH = 128
DEC_SEQ = 4
PAST_LEN = 16384
PAGE_SIZE = 128

N_EVEN = (DEPTH + 1) // 2
N_ODD = DEPTH // 2

POOL_WINDOWS = (2, 4, 8, 16)
POOL_GROUPS = len(POOL_WINDOWS)
POOL_WIDTH = D_MODEL // 4
POOL_GROUP_DIM = POOL_WIDTH // POOL_GROUPS
POOL_HIST = max(POOL_WINDOWS) - 1
RET_WIDTH = D_MODEL - POOL_WIDTH
RET_HEADS = 6
RET_DV = RET_WIDTH // RET_HEADS
RET_DK = RET_DV // 2
RET_QK = RET_HEADS * RET_DK
RET_CHUNK = 128
ROPE_BASE = 10000.0
SG_WIDTH = D_MODEL // 2
SG_CHUNK = 128
SG_GROUPS = 4
SG_GROUP_DIM = SG_WIDTH // SG_GROUPS
CONV_CH = D_MODEL // 2
CONV_K = 31
CONV_HIST = CONV_K - 1
D_FF = 4 * D_MODEL
EPS = 1e-6

EVEN_IN = POOL_WIDTH + 2 * RET_QK + 2 * RET_WIDTH
ODD_IN = 2 * SG_WIDTH + 2 * CONV_CH

kernel_name = "pool_retention_sgmlp_conv_hybrid_step"


def _rms(x, g):
    xf = x.astype(jnp.float32)
    y = xf * lax.rsqrt(jnp.mean(xf * xf, axis=-1, keepdims=True) + EPS)
    return (y * g.astype(jnp.float32)).astype(x.dtype)


def _ln(x, g, b):
    xf = x.astype(jnp.float32)
    xc = xf - jnp.mean(xf, axis=-1, keepdims=True)
    y = xc * lax.rsqrt(jnp.mean(xc * xc, axis=-1, keepdims=True) + EPS)
    return (y * g.astype(jnp.float32) + b.astype(jnp.float32)).astype(x.dtype)


def _rotary(x, pos):
    half = x.shape[-1] // 2
    inv = ROPE_BASE ** (-jnp.arange(half, dtype=jnp.float32) / half)
    ang = pos.astype(jnp.float32)[:, None] * inv[None, :]
    cos = jnp.cos(ang)[None, :, None, :]
    sin = jnp.sin(ang)[None, :, None, :]
    x1, x2 = x[..., :half], x[..., half:]
    return jnp.concatenate([x1 * cos - x2 * sin, x1 * sin + x2 * cos], axis=-1)


def _pool_mixer(p, hist, pos0, w_pool, s_pool):
    B, T, C = p.shape
    ext = jnp.concatenate([hist, p], axis=1)
    extf = ext.astype(jnp.float32)
    cs = jnp.concatenate([jnp.zeros((B, 1, C), jnp.float32), jnp.cumsum(extf, axis=1)], axis=1)
    end = cs[:, POOL_HIST + 1:]
    pos = pos0 + jnp.arange(T)
    means = []
    for g, w in enumerate(POOL_WINDOWS):
        sl = slice(g * POOL_GROUP_DIM, (g + 1) * POOL_GROUP_DIM)
        start = cs[:, POOL_HIST + 1 - w:POOL_HIST + 1 - w + T, sl]
        cnt = jnp.minimum(w, pos + 1).astype(jnp.float32)[None, :, None]
        means.append((end[..., sl] - start) / cnt)
    d = (jnp.concatenate(means, axis=-1) - extf[:, POOL_HIST:]).reshape(B, T, POOL_GROUPS, POOL_GROUP_DIM)
    y = jnp.einsum('btgc,gcd->btgd', d, w_pool.astype(jnp.float32)).reshape(B, T, C)
    y = y * s_pool.astype(jnp.float32)
    return y.astype(p.dtype), ext[:, -POOL_HIST:]


def _retention(q, k, v, s0):
    B, T, H, dk = q.shape
    dv = v.shape[-1]
    L = RET_CHUNK if T % RET_CHUNK == 0 else T
    n = T // L
    log_g = jnp.log1p(-jnp.exp2(-5.0 - jnp.arange(H, dtype=jnp.float32)))
    idx = jnp.arange(L, dtype=jnp.float32)
    diff = idx[:, None] - idx[None, :]
    mask = jnp.where(diff[None] >= 0, jnp.exp(log_g[:, None, None] * jnp.maximum(diff, 0.0)[None]), 0.0)
    q_decay = jnp.exp(log_g[:, None] * (idx + 1.0))[None, :, :, None]
    k_decay = jnp.exp(log_g[:, None] * (L - 1.0 - idx))[None, :, :, None]
    chunk_decay = jnp.exp(log_g * L)[None, :, None, None]

    def to_chunks(a):
        return a.reshape(B, n, L, H, a.shape[-1]).transpose(1, 0, 3, 2, 4)

    def step(s, inp):
        qc, kc, vc = inp
        scores = jnp.einsum('bhld,bhmd->bhlm', qc, kc) * mask[None]
        o = jnp.einsum('bhlm,bhme->bhle', scores, vc) + jnp.einsum('bhld,bhde->bhle', qc * q_decay, s)
        s = s * chunk_decay + jnp.einsum('bhld,bhle->bhde', kc * k_decay, vc)
        return s, o

    s, o = lax.scan(step, s0, (to_chunks(q), to_chunks(k), to_chunks(v)))
    o = o.transpose(1, 0, 3, 2, 4).reshape(B, T, H, dv)
    return o, s


def _even_mixer(h, pos0, pool_hist, ret_state, w_in, w_pool, s_pool, w_out):
    B, T, _ = h.shape
    z = h @ w_in
    p, q, k, v, g = jnp.split(z, [POOL_WIDTH, POOL_WIDTH + RET_QK, POOL_WIDTH + 2 * RET_QK,
                                  POOL_WIDTH + 2 * RET_QK + RET_WIDTH], axis=-1)
    ya, new_hist = _pool_mixer(p, pool_hist, pos0, w_pool, s_pool)
    pos = pos0 + jnp.arange(T)
    q = _rotary(q.reshape(B, T, RET_HEADS, RET_DK).astype(jnp.float32), pos)
    k = _rotary(k.reshape(B, T, RET_HEADS, RET_DK).astype(jnp.float32), pos) * (RET_DK ** -0.5)
    v = v.reshape(B, T, RET_HEADS, RET_DV).astype(jnp.float32)
    o, s = _retention(q, k, v, ret_state.astype(jnp.float32))
    o = o * lax.rsqrt(jnp.mean(o * o, axis=-1, keepdims=True) + EPS)
    yb = (jax.nn.silu(g.astype(jnp.float32)) * o.reshape(B, T, RET_WIDTH)).astype(h.dtype)
    y = jnp.concatenate([ya, yb], axis=-1) @ w_out
    return y, new_hist, s.astype(ret_state.dtype)


def _odd_mixer(h, conv_hist, w_in, sg_ln_g, sg_ln_b, sg_w, sg_b, dw_w, dw_b, cv_ln_g, cv_ln_b, w_out):
    B, T, _ = h.shape
    z = h @ w_in
    u, v, a, gate = jnp.split(z, [SG_WIDTH, 2 * SG_WIDTH, 2 * SG_WIDTH + CONV_CH], axis=-1)
    u = jax.nn.gelu(u)
    v = _ln(jax.nn.gelu(v), sg_ln_g, sg_ln_b)
    L = SG_CHUNK if T % SG_CHUNK == 0 else T
    n = T // L
    causal = jnp.tril(jnp.ones((L, L), dtype=bool))
    ws = jnp.where(causal[None], sg_w[:, :L, :L].astype(jnp.float32), 0.0)
    vc = v.reshape(B, n, L, SG_GROUPS, SG_GROUP_DIM).astype(jnp.float32)
    mixed = jnp.einsum('gij,bnjgc->bnigc', ws, vc) + sg_b[:, :L].astype(jnp.float32).T[None, None, :, :, None]
    yc = (u.astype(jnp.float32) * mixed.reshape(B, T, SG_WIDTH)).astype(h.dtype)
    glu = a * jax.nn.sigmoid(gate)
    ext = jnp.concatenate([conv_hist.astype(glu.dtype), glu], axis=1)
    conv = lax.conv_general_dilated(ext, dw_w[:, None, :].astype(ext.dtype), window_strides=(1,),
                                    padding='VALID', dimension_numbers=('NWC', 'WIO', 'NWC'),
                                    feature_group_count=CONV_CH) + dw_b
    yd = jax.nn.silu(_ln(conv, cv_ln_g, cv_ln_b))
    y = jnp.concatenate([yc, yd.astype(h.dtype)], axis=-1) @ w_out
    return y, ext[:, -CONV_HIST:], v


def _trunk(x, pos0, pool_hist, ret_state, conv_hist, w):
    (norm_mix_pre, norm_mix_post, norm_ffn_pre, norm_ffn_post, w_in_even, w_pool, s_pool, w_out_even,
     w_in_odd, sg_ln_g, sg_ln_b, sg_w, sg_b, dw_w, dw_b, cv_ln_g, cv_ln_b, w_out_odd, w_up, w_down) = w
    h = x
    pools, rets, convs, sgvs = [], [], [], []
    for l in range(DEPTH):
        i = l // 2
        a = _rms(h, norm_mix_pre[l])
        if l % 2 == 0:
            y, ph, rs = _even_mixer(a, pos0, pool_hist[i], ret_state[i], w_in_even[i], w_pool[i],
                                    s_pool[i], w_out_even[i])
            pools.append(ph)
            rets.append(rs)
        else:
            y, ch, sv = _odd_mixer(a, conv_hist[i], w_in_odd[i], sg_ln_g[i], sg_ln_b[i], sg_w[i], sg_b[i],
                                   dw_w[i], dw_b[i], cv_ln_g[i], cv_ln_b[i], w_out_odd[i])
            convs.append(ch)
            sgvs.append(sv)
        h = h + _rms(y, norm_mix_post[l])
        f = _rms(h, norm_ffn_pre[l])
        f = jnp.square(jax.nn.relu(f @ w_up[l])) @ w_down[l]
        h = h + _rms(f, norm_ffn_post[l])
    return h, jnp.stack(pools), jnp.stack(rets), jnp.stack(convs), jnp.stack(sgvs)


def setup_inputs(seed: int = 0) -> dict:
    key = jax.random.key(seed)
    ks = jax.random.split(key, 32)
    f32 = jnp.float32

    def nrm(k, shape, scale):
        return jax.random.normal(k, shape, f32) * scale

    def gain(k, shape):
        return 1.0 + 0.05 * jax.random.normal(k, shape, f32)

    return {
        "x_prompt": nrm(ks[0], (BATCH, SEQ, D_MODEL), 1.0),
        "x_sample": nrm(ks[1], (DEC_BATCH, DEC_SEQ, D_MODEL), 1.0),
        "state_pool": nrm(ks[2], (N_EVEN, DEC_BATCH, POOL_HIST, POOL_WIDTH), 1.0),
        "state_ret": nrm(ks[3], (N_EVEN, DEC_BATCH, RET_HEADS, RET_DK, RET_DV), 0.5),
        "state_conv": nrm(ks[4], (N_ODD, DEC_BATCH, CONV_HIST, CONV_CH), 1.0),
        "norm_mix_pre": gain(ks[5], (DEPTH, D_MODEL)),
        "norm_mix_post": gain(ks[6], (DEPTH, D_MODEL)),
        "norm_ffn_pre": gain(ks[7], (DEPTH, D_MODEL)),
        "norm_ffn_post": gain(ks[8], (DEPTH, D_MODEL)),
        "w_in_even": nrm(ks[9], (N_EVEN, D_MODEL, EVEN_IN), D_MODEL ** -0.5),
        "w_pool": nrm(ks[10], (N_EVEN, POOL_GROUPS, POOL_GROUP_DIM, POOL_GROUP_DIM), POOL_GROUP_DIM ** -0.5),
        "s_pool": gain(ks[11], (N_EVEN, POOL_WIDTH)),
        "w_out_even": nrm(ks[12], (N_EVEN, POOL_WIDTH + RET_WIDTH, D_MODEL), (POOL_WIDTH + RET_WIDTH) ** -0.5),
        "w_in_odd": nrm(ks[13], (N_ODD, D_MODEL, ODD_IN), D_MODEL ** -0.5),
        "sg_ln_g": gain(ks[14], (N_ODD, SG_WIDTH)),
        "sg_ln_b": nrm(ks[15], (N_ODD, SG_WIDTH), 0.02),
        "sg_w": nrm(ks[16], (N_ODD, SG_GROUPS, SG_CHUNK, SG_CHUNK), SG_CHUNK ** -0.5),
        "sg_b": gain(ks[17], (N_ODD, SG_GROUPS, SG_CHUNK)),
        "dw_w": nrm(ks[18], (N_ODD, CONV_K, CONV_CH), CONV_K ** -0.5),
        "dw_b": nrm(ks[19], (N_ODD, CONV_CH), 0.02),
        "cv_ln_g": gain(ks[20], (N_ODD, CONV_CH)),
        "cv_ln_b": nrm(ks[21], (N_ODD, CONV_CH), 0.02),
        "w_out_odd": nrm(ks[22], (N_ODD, SG_WIDTH + CONV_CH, D_MODEL), (SG_WIDTH + CONV_CH) ** -0.5),
        "w_up": nrm(ks[23], (DEPTH, D_MODEL, D_FF), D_MODEL ** -0.5),
        "w_down": nrm(ks[24], (DEPTH, D_FF, D_MODEL), D_FF ** -0.5),
    }


def reference(x_prompt, x_sample, state_pool, state_ret, state_conv, norm_mix_pre, norm_mix_post,
              norm_ffn_pre, norm_ffn_post, w_in_even, w_pool, s_pool, w_out_even, w_in_odd, sg_ln_g,
              sg_ln_b, sg_w, sg_b, dw_w, dw_b, cv_ln_g, cv_ln_b, w_out_odd, w_up, w_down):
    w = (norm_mix_pre, norm_mix_post, norm_ffn_pre, norm_ffn_post, w_in_even, w_pool, s_pool, w_out_even,
         w_in_odd, sg_ln_g, sg_ln_b, sg_w, sg_b, dw_w, dw_b, cv_ln_g, cv_ln_b, w_out_odd, w_up, w_down)
    dt = x_prompt.dtype
    pool0 = jnp.zeros((N_EVEN, BATCH, POOL_HIST, POOL_WIDTH), dt)
    ret0 = jnp.zeros((N_EVEN, BATCH, RET_HEADS, RET_DK, RET_DV), state_ret.dtype)
    conv0 = jnp.zeros((N_ODD, BATCH, CONV_HIST, CONV_CH), dt)
    y_prompt, pool_p, ret_p, conv_p, _ = _trunk(x_prompt, 0, pool0, ret0, conv0, w)
    y_sample, pool_s, ret_s, conv_s, sgv_s = _trunk(x_sample, PAST_LEN, state_pool, state_ret, state_conv, w)
    return (y_prompt, y_sample, pool_p, pool_s, ret_p, ret_s, conv_p, conv_s, sgv_s)
```

```python
import numpy as np
import concourse.bass as bass
import concourse.mybir as mybir
from concourse.bass_utils import run_bass_kernel_spmd

F32 = mybir.dt.float32
BF16 = mybir.dt.bfloat16
AF = mybir.ActivationFunctionType
ALU = mybir.AluOpType

D = 2048
DEPTH = 4
SEQ = 2048
NPT = SEQ // 128
NS = 64
HEADS = 6
EPS = 1e-6
POOLW = (2, 4, 8, 16)
ENGS = ("pe", "act", "dve", "pool", "sp")
RING = {"sp": 12, "pool": 6}
BLK = 256


class Prog:
    def __init__(self):
        self.ops = {e: [] for e in ENGS}
        self.clock = {e: {f: -1 for f in ENGS} for e in ENGS}
        self.dknown = {e: set() for e in ENGS}
        self.snap = {}
        self.state = {}
        self.ndma = {"sp": 0, "pool": 0}
        self.signal = set()
        self.cache = {}

    def keys(self, x):
        if isinstance(x, str):
            return (x,)
        name = x.tensor.name
        if name != "SB":
            return (name,)
        esz = 4 if x.dtype == F32 else 2
        ck = (x.offset, x.ap, esz)
        r = self.cache.get(ck)
        if r is not None:
            return r
        ap = x.ap
        pstep = ap[0][0]
        off = x.offset % pstep
        dims = [(s, c) for (s, c) in ap[1:] if c > 1]
        if not dims:
            ivs = [(off, off + 1)]
        else:
            inner = dims[-1]
            outer = dims[:-1]
            ncomb = 1
            for s, c in outer:
                ncomb *= c
            if inner[0] == 1 and ncomb <= 512:
                starts = [off]
                for s, c in outer:
                    starts = [b + s * i for b in starts for i in range(c)]
                ivs = [(b, b + inner[1]) for b in starts]
            else:
                hi = off + sum(s * (c - 1) for s, c in dims) + 1
                ivs = [(off, hi)]
        ks = set()
        for lo, hi in ivs:
            for b in range((lo * esz) // BLK, ((hi * esz) - 1) // BLK + 1):
                ks.add(b)
        r = tuple(ks)
        self.cache[ck] = r
        return r

    dead = False
    maxops = 0

    def op(self, eng, fn, reads=(), writes=(), dma=False):
        if self.dead:
            return None
        self.nops = getattr(self, "nops", 0) + 1
        if self.maxops and self.nops > self.maxops:
            self.dead = True
            return None
        deps = set()
        rk = [k for x in reads for k in self.keys(x)]
        wk = [k for x in writes for k in self.keys(x)]
        psr = [k for k in rk if isinstance(k, str) and k.startswith("PS")]
        if psr:
            rk = [k for k in rk if not (isinstance(k, str) and k.startswith("PS"))]
            wk = wk + psr
        for k in rk:
            st = self.state.get(k)
            if st and st[0] is not None:
                deps.add(st[0])
        for k in wk:
            st = self.state.get(k)
            if st:
                if st[0] is not None:
                    deps.add(st[0])
                deps.update(st[1].values())
        idx = len(self.ops[eng])
        if dma:
            n = self.ndma[eng]
            self.ndma[eng] = n + 1
            tok = ("d", eng, n)
            if n >= RING[eng]:
                deps.add(("d", eng, n - RING[eng]))
        else:
            tok = ("c", eng, idx)
        clk = self.clock[eng]
        best = {}
        ddeps = []
        for dtok in deps:
            if dtok[0] == "c":
                _, f, i = dtok
                if f == eng and eng in ("pe", "sp"):
                    continue
                if clk[f] >= i:
                    continue
                if f not in best or best[f] < i:
                    best[f] = i
            else:
                if dtok not in self.dknown[eng]:
                    ddeps.append(dtok)
        waits = [("c", f, i) for f, i in best.items()] + ddeps
        for dtok in waits:
            if dtok[0] == "c":
                self.signal.add(dtok)
                sn = self.snap[dtok]
                for f in ENGS:
                    if sn[f] > clk[f]:
                        clk[f] = sn[f]
                if dtok[2] > clk[dtok[1]]:
                    clk[dtok[1]] = dtok[2]
            else:
                self.dknown[eng].add(dtok)
                sn = self.snap[dtok]
                for f in ENGS:
                    if sn[f] > clk[f]:
                        clk[f] = sn[f]
        sn = dict(clk)
        self.snap[tok] = sn
        self.ops[eng].append((waits, fn, tok))
        for k in rk:
            st = self.state.setdefault(k, [None, {}])
            if tok[0] == "c":
                st[1][(tok[0], tok[1])] = tok
            else:
                st[1][tok] = tok
        for k in wk:
            self.state[k] = [tok, {}]
        return tok

    def emit(self, nc, block, sems, dsems):
        semval = {}
        for e in ENGS:
            cnt = 0
            for i, (_, _, tok) in enumerate(self.ops[e]):
                if tok in self.signal:
                    cnt += 1
                    semval[tok] = cnt

        def run(e, engobj):
            for waits, fn, tok in self.ops[e]:
                for w in waits:
                    if w[0] == "c":
                        engobj.wait_ge(sems[w[1]], semval[w])
                    else:
                        r = RING[w[1]]
                        engobj.wait_ge(dsems[w[1]][w[2] % r], 16 * (w[2] // r + 1))
                if fn is None:
                    continue
                ins = fn(engobj)
                if tok[0] == "d":
                    ins.then_inc(dsems[tok[1]][tok[2] % RING[tok[1]]], 16)
                elif tok in self.signal:
                    ins.then_inc(sems[e], 1)

        block.tensor(lambda g: run("pe", g))
        block.scalar(lambda g: run("act", g))
        block.vector(lambda g: run("dve", g))
        block.gpsimd(lambda g: run("pool", g))
        block.sync(lambda g: run("sp", g))


def host_tables(NPT=NPT):
    SEQ = NPT * 128
    half = 64
    inv = (10000.0 ** (-np.arange(half, dtype=np.float32) / np.float32(half))).astype(np.float32)
    pos_p = np.arange(SEQ, dtype=np.float32)
    pos_s = (16384 + np.arange(4)).astype(np.float32)
    pos_s = np.tile(pos_s, 16)
    ang_p = (pos_p[:, None] * inv[None, :]).astype(np.float32)
    ang_s = (pos_s[:, None] * inv[None, :]).astype(np.float32)
    cosT = np.zeros((128, NPT + 1, 64), np.float32)
    sinT = np.zeros((128, NPT + 1, 64), np.float32)
    cosT[:, :NPT] = np.cos(ang_p).astype(np.float32).reshape(NPT, 128, 64).transpose(1, 0, 2)
    sinT[:, :NPT] = np.sin(ang_p).astype(np.float32).reshape(NPT, 128, 64).transpose(1, 0, 2)
    cosT[:64, NPT] = np.cos(ang_s).astype(np.float32)
    sinT[:64, NPT] = np.sin(ang_s).astype(np.float32)
    hh = np.arange(HEADS, dtype=np.float32)
    log_g = np.log1p(-np.exp2(-5.0 - hh)).astype(np.float32)
    scale = np.float32(128 ** -0.5)
    idx = np.arange(128, dtype=np.float32)
    diff = idx[:, None] - idx[None, :]
    mask = np.where(diff[None] >= 0, np.exp(log_g[:, None, None] * np.maximum(diff, 0.0)[None]), 0.0)
    maskT = (mask.transpose(2, 0, 1) * scale).astype(np.float32)
    i4 = np.arange(4, dtype=np.float32)
    d4 = i4[:, None] - i4[None, :]
    m4 = np.where(d4[None] >= 0, np.exp(log_g[:, None, None] * np.maximum(d4, 0.0)[None]), 0.0)
    maskS = np.zeros((64, HEADS, 64), np.float32)
    for s in range(16):
        maskS[4 * s:4 * s + 4, :, 4 * s:4 * s + 4] = m4.transpose(2, 0, 1) * scale
    qdec = np.exp(log_g[:, None] * (idx + 1.0)[None, :]).astype(np.float32)
    qdecT = np.broadcast_to(qdec[None], (128, HEADS, 128)).copy()
    qdec4 = np.exp(log_g[:, None] * (i4 + 1.0)[None, :]).astype(np.float32)
    qdecS = np.broadcast_to(np.tile(qdec4, (1, 16))[None], (128, HEADS, 64)).copy()
    kdec = (np.exp(log_g[None, :] * (127.0 - idx)[:, None]) * scale).astype(np.float32)
    kdec4 = (np.exp(log_g[None, :] * (3.0 - i4)[:, None]) * scale).astype(np.float32)
    kdecS = np.zeros((128, HEADS), np.float32)
    kdecS[:64] = np.tile(kdec4, (16, 1))
    colmask = np.zeros((128, 16, 64), np.float32)
    onehot = np.zeros((128, 16), np.float32)
    for s in range(16):
        colmask[:, s, 4 * s:4 * s + 4] = 1.0
        onehot[4 * s:4 * s + 4, s] = 1.0
    invc = np.zeros((128, 4, 16), np.float32)
    for g, w in enumerate(POOLW):
        invc[:, g, :] = 1.0 / np.minimum(w, np.arange(16) + 1.0)
    tri = np.tril(np.ones((128, 128), np.float32)).T.copy()
    bd = np.zeros((128, 64), np.float32)
    for s in range(16):
        for j in range(4):
            for i in range(j, 4):
                bd[4 * s + j, 4 * s + i] = 1.0
    cd = np.exp(log_g * 128.0).astype(np.float32)
    cd4 = np.exp(log_g * 4.0).astype(np.float32)
    return dict(cosT=cosT, sinT=sinT, maskT=maskT, maskS=np.concatenate([maskS, np.zeros_like(maskS)], 0),
                qdecT=qdecT, qdecS=qdecS, kdec=kdec, kdecS=kdecS, colmask=colmask, onehot=onehot,
                invc=invc, tri=tri, bd=bd), cd, cd4


def table_shapes(NPT):
    return dict(cosT=[128, NPT + 1, 64], sinT=[128, NPT + 1, 64], maskT=[128, HEADS, 128], maskS=[128, HEADS, 64],
                qdecT=[128, HEADS, 128], qdecS=[128, HEADS, 64], kdec=[128, HEADS], kdecS=[128, HEADS],
                    colmask=[128, 16, 64], onehot=[128, 16], invc=[128, 4, 16], tri=[128, 128], bd=[128, 64])


def build(cd, cd4, DEPTH=DEPTH, NPT=NPT):
    SEQ = NPT * 128
    NE = (DEPTH + 1) // 2
    NO = max(DEPTH // 2, 1)
    nc = bass.Bass("TRN2", target_bir_lowering=False)
    P = Prog()

    def din(name, shape):
        return nc.dram_tensor(name, list(shape), F32, kind="ExternalInput").ap()

    def dout(name, shape):
        return nc.dram_tensor(name, list(shape), F32, kind="ExternalOutput").ap()

    xp = din("xp", [SEQ, D]); xs = din("xs", [NS, D])
    spool = din("spool", [NE, 240, 512]); sret = din("sret", [NE, 16, HEADS, 128, 256]); sconv = din("sconv", [NO, 480, 1024])
    norms = din("norms", [4 * DEPTH, D])
    w_in_even = din("w_in_even", [NE, D, 5120]); w_pool = din("w_pool", [NE, 4, 128, 128]); s_pool = din("s_pool", [NE, 512])
    w_out_even = din("w_out_even", [NE, D, D]); w_in_odd = din("w_in_odd", [NO, D, 4096])
    sg_ln_g = din("sg_ln_g", [NO, 1024]); sg_ln_b = din("sg_ln_b", [NO, 1024]); sg_w = din("sg_w", [NO, 4, 128, 128]); sg_b = din("sg_b", [NO, 4, 128])
    dw_w = din("dw_w", [NO, 31, 1024]); dw_b = din("dw_b", [NO, 1024]); cv_ln_g = din("cv_ln_g", [NO, 1024]); cv_ln_b = din("cv_ln_b", [NO, 1024])
    w_out_odd = din("w_out_odd", [NO, D, D]); w_up = din("w_up", [DEPTH, D, 8192]); w_down = din("w_down", [DEPTH, 8192, D])
    TABLE_SHAPES = table_shapes(NPT)
    tabs = {k: din("t_" + k, v) for k, v in TABLE_SHAPES.items()}
    yp = dout("yp", [SEQ, D]); ys = dout("ys", [NS, D])
    o_pp = dout("o_pp", [NE, 15, 512]); o_ps = dout("o_ps", [NE, 240, 512])
    o_rp = dout("o_rp", [NE, HEADS, 128, 256]); o_rs = dout("o_rs", [NE, 16, HEADS, 128, 256])
    o_cp = dout("o_cp", [NO, 30, 1024]); o_cs = dout("o_cs", [NO, 480, 1024]); o_sv = dout("o_sv", [NO, NS, 1024])
    hscr = nc.dram_tensor("hscr", [NPT + 1, 128, D], F32).ap()

    SBW = 103 * 1024
    import contextlib
    es = contextlib.ExitStack()
    SB = es.enter_context(nc.sbuf_tensor("SB", [128, SBW], BF16))
    banks = [es.enter_context(nc.psum_tensor("PS%d" % i, [128, 512], F32)) for i in range(8)]
    sems = {e: es.enter_context(nc.semaphore("s_" + e)) for e in ENGS}
    dsems = {q: [es.enter_context(nc.semaphore("d_%s%d" % (q, i))) for i in range(RING[q])] for q in RING}
    nc.allow_low_precision("bf16 matmul operands, fp32 accumulation")
    es.enter_context(nc.allow_non_contiguous_dma("small strided parameter loads"))

    import os
    kstop = os.environ.get("KSTOP", "")

    P.maxops = int(os.environ.get("KMAXOPS", "0"))

    def stage(name):
        if os.environ.get("KVERBOSE"):
            print("stage", name, "nops", getattr(P, "nops", 0))
        if kstop and name == kstop:
            P.dead = True

    cur = [0]

    def alloc(dt, *dims):
        n = 1
        for d_ in dims:
            n *= d_
        nb = n * (4 if dt == F32 else 2)
        nb = (nb + BLK - 1) // BLK * BLK
        off = cur[0]
        cur[0] += nb
        assert cur[0] <= SBW * 2, "SBUF overflow"
        v = SB[:, off // 2:(off + nb) // 2]
        if dt == F32:
            v = v.bitcast(F32)
        v = v[:, 0:n]
        if len(dims) == 2:
            v = v.rearrange("p (a b) -> p a b", a=dims[0])
        elif len(dims) == 3:
            v = v.rearrange("p (a b c) -> p a b c", a=dims[0], b=dims[1])
        return v

    pb = [0]

    held = set()

    def psum():
        k = pb[0] % 6
        pb[0] += 1
        return banks[k]

    gpb = [0]

    def gpsum():
        k = 6 + gpb[0] % 2
        gpb[0] += 1
        return banks[k]

    def hold(n):
        out = []
        for _ in range(n):
            while pb[0] % 8 in held:
                pb[0] += 1
            k = pb[0] % 8
            pb[0] += 1
            held.add(k)
            out.append(banks[k])
        return out

    def release(bs):
        for b in bs:
            held.discard(banks.index(b))

    def mm(out, lhsT, rhs, start=True, stop=True):
        P.op("pe", lambda e: e.matmul(out, lhsT, rhs, start=start, stop=stop), reads=[lhsT, rhs], writes=[out])

    def tr(out, in_, ident):
        P.op("pe", lambda e: e.transpose(out, in_, ident), reads=[in_, ident], writes=[out])

    def act(out, in_, func, bias=None, scale=None, accum=None):
        kw = {}
        rd = [in_]
        if bias is not None:
            kw["bias"] = bias
            if not isinstance(bias, float):
                rd.append(bias)
        if scale is not None:
            kw["scale"] = scale
            if not isinstance(scale, float):
                rd.append(scale)
        wr = [out]
        if accum is not None:
            kw["accum_out"] = accum
            wr.append(accum)
            P.op("dve", lambda e: e.memset(accum, 0.0), writes=[accum])
        P.op("act", lambda e: e.activation(out, in_, func, **kw), reads=rd, writes=wr)

    def tt(out, in0, in1, op, eng="dve"):
        P.op(eng, lambda e: e.tensor_tensor(out, in0, in1, op), reads=[in0, in1], writes=[out])

    def ts(out, in0, s1, s2, op0, op1=None, eng="dve"):
        rd = [in0] + [s for s in (s1, s2) if s is not None and not isinstance(s, float)]
        if op1 is None:
            P.op(eng, lambda e: e.tensor_scalar(out, in0, s1, None, op0), reads=rd, writes=[out])
        else:
            P.op(eng, lambda e: e.tensor_scalar(out, in0, s1, s2, op0, op1), reads=rd, writes=[out])

    def stt(out, in0, scalar, in1, op0, op1, eng="dve"):
        rd = [in0, in1] + ([] if isinstance(scalar, float) else [scalar])
        P.op(eng, lambda e: e.scalar_tensor_tensor(out, in0, scalar, in1, op0, op1), reads=rd, writes=[out])

    def cp(out, in_, eng="dve"):
        if eng == "act":
            P.op("act", lambda e: e.activation(out, in_, AF.Identity), reads=[in_], writes=[out])
        else:
            P.op(eng, lambda e: e.tensor_copy(out, in_), reads=[in_], writes=[out])

    def recip(out, in_):
        P.op("dve", lambda e: e.reciprocal(out, in_), reads=[in_], writes=[out])

    def memset(ap, val, eng="dve"):
        P.op(eng, lambda e: e.memset(ap, val), writes=[ap])

    def dma(out, in_, q="sp", rk=(), wk=()):
        rd = list(rk) + ([in_] if in_.tensor.name == "SB" else [])
        wr = list(wk) + ([out] if out.tensor.name == "SB" else [])
        return P.op(q, lambda e: e.dma_start(out=out, in_=in_), reads=rd, writes=wr, dma=True)

    for z0 in range(0, SBW, 16384):
        z1 = min(SBW, z0 + 16384)
        memset(SB[:, z0:z1], 0.0, eng="pool" if (z0 // 16384) % 2 else "dve")
    identf = alloc(F32, 128)
    identb = alloc(BF16, 128)
    P.op("pool", lambda e: e.memset(identf, 0.0), writes=[identf])
    P.op("pool", lambda e: e.affine_select(out=identf, in_=identf, compare_op=ALU.not_equal, fill=1.0, base=0,
                                           pattern=[[-1, 128]], channel_multiplier=1), reads=[identf], writes=[identf])
    cp(identb, identf)
    T = {}
    for k, shp in TABLE_SHAPES.items():
        T[k] = alloc(F32, *shp[1:])
        dma(T[k], tabs[k])
    maskTb = alloc(BF16, HEADS, 128); cp(maskTb, T["maskT"])
    maskSb = alloc(BF16, HEADS, 64); cp(maskSb, T["maskS"])
    cosv = T["cosT"]; sinv = T["sinT"]

    stage("init")
    hb = alloc(F32, D); yacc = alloc(F32, D); abf = alloc(BF16, D); junk = alloc(F32, 1024)
    xT = alloc(BF16, 16, 128); yT = alloc(BF16, 16, 128); yin = alloc(BF16, D)
    gbc = [alloc(F32, D)]
    xTfa = alloc(BF16, 16, 256); xTf = [xTfa[:, :, 0:128], xTfa[:, :, 128:256]]
    yaccF = [alloc(F32, D) for _ in range(2)]; junkF = alloc(F32, 512); hT = alloc(BF16, 8, 256)
    gn = [0]
    WR = [alloc(BF16, 4096) for _ in range(4)]
    wn = [0]
    small = alloc(F32, 64)
    zbuf = alloc(F32, 4096)
    rt = [alloc(F32, 6, 64) for _ in range(4)]
    hstg = alloc(F32, 8, 128)
    base1 = cur[0]
    S = alloc(F32, HEADS, 256); Sb = alloc(BF16, HEADS, 256)
    pext = alloc(F32, 4, 144); wpb = alloc(BF16, 4, 128); spc = alloc(F32, 4)
    e1 = cur[0]
    cur[0] = base1
    gext = alloc(F32, 8, 160); dwT = alloc(F32, 8, 31); dwbT = alloc(F32, 8)
    lnt = alloc(F32, 1024)
    wsT = alloc(BF16, 4, 128); wsTs = alloc(BF16, 4, 64); sgw = alloc(F32, 4, 128); sgbP = alloc(F32, 4); sgbS = alloc(F32, 4)
    base2 = max(e1, cur[0])
    cur[0] = base2
    qk = alloc(BF16, 1536); vt = alloc(BF16, 1536)
    qT = alloc(BF16, HEADS, 128); qdT = alloc(BF16, HEADS, 128); kT = alloc(BF16, HEADS, 128); kd = alloc(BF16, HEADS, 128)
    PT = alloc(BF16, HEADS, 128)
    S0 = zbuf[:, 2048:3584].rearrange("p (h e) -> p h e", h=HEADS); S0b = alloc(BF16, HEADS, 256); qdm = alloc(BF16, HEADS, 64); kdm = alloc(BF16, HEADS * 128)
    pextS = alloc(F32, 4, 304); pA = alloc(F32, 304); pB = alloc(F32, 304)
    dT = alloc(BF16, 4, 128); hloadE = alloc(F32, 512)
    e2 = cur[0]
    cur[0] = base2
    gextS = alloc(F32, 8, 16 * 34); cacc = alloc(F32, 8, 128); hloadO = alloc(F32, 1024)
    e3 = cur[0]
    cur[0] = base2
    pass
    cur[0] = max(e2, e3, cur[0])

    print("SBUF bytes used", cur[0], "of", SBW * 2)

    def gain(j, l_):
        b = gbc[0]
        gn[0] += 1
        r = DEPTH * j + l_
        dma(b, norms[r:r + 1, :].to_broadcast([128, D]))
        return b

    def col(i):
        return small[:, i:i + 1]

    wscr = [nc.dram_tensor("wscr%d" % l_, [96, 128, 4096], BF16).ap() for l_ in range(DEPTH)]
    wids = {}
    wcnt = [0] * DEPTH
    curl = [0]

    def wtile(W, r0, c0, nk, ncols):
        b = WR[wn[0] % len(WR)]
        wn[0] += 1
        bv = b.rearrange("p (k n) -> p k n", k=nk)
        key = (W.tensor.name, W.offset, r0, c0, nk)
        if key not in wids:
            wid = (curl[0], wcnt[curl[0]])
            wcnt[curl[0]] += 1
            wids[key] = wid
            dma(bv, W[r0:r0 + nk * 128, c0:c0 + ncols].rearrange("(k p) n -> p k n", p=128), q="pool")
            dma(wscr[wid[0]][wid[1]], b, q="sp", wk=["wscr%d_%d" % wid])
        else:
            wid = wids[key]
            dma(b, wscr[wid[0]][wid[1]], q="pool", rk=["wscr%d_%d" % wid])
        return bv

    def rstd_of(ss_col, out_col, n):
        act(out_col, ss_col, AF.Sqrt, bias=EPS, scale=1.0 / n)
        recip(out_col, out_col)

    def transposes_to(dst, src_bf, nchunks, R, dst_chunk0=0):
        c = 0
        while c < nchunks:
            n = min(8, nchunks - c)
            pst = psum()
            psv = pst[:, :].bitcast(BF16).rearrange("p (a b) -> p a b", a=8)
            for j in range(n):
                tr(psv[:, j, 0:R], src_bf[0:R, (c + j) * 128:(c + j + 1) * 128], identb[0:R, 0:R])
            cp(dst[:, dst_chunk0 + c:dst_chunk0 + c + n, 0:R], psv[:, 0:n, 0:R], eng="act")
            c += n

    def tr32_multi(pairs):
        for b0 in range(0, len(pairs), 4):
            grp = pairs[b0:b0 + 4]
            pst = psum()
            v = pst[:, :].bitcast(BF16).rearrange("p (a b) -> p a b", a=8)
            for j, (src, dst) in enumerate(grp):
                pr, w = src.shape
                hi = abf[0:pr, j * 128:j * 128 + w]
                lo = abf[0:pr, 1024 + j * 128:1024 + j * 128 + w]
                tmp = yacc[0:pr, j * 128:j * 128 + w]
                cp(hi, src)
                tt(tmp, src, hi, ALU.subtract)
                cp(lo, tmp)
                tr(v[0:w, j, 0:pr], hi, identb[0:pr, 0:pr])
                tr(v[0:w, 4 + j, 0:pr], lo, identb[0:pr, 0:pr])
            for j, (src, dst) in enumerate(grp):
                pr, w = src.shape
                cp(dst, v[0:w, j, 0:pr])
                tt(dst, dst, v[0:w, 4 + j, 0:pr], ALU.add)

    def rms_to_xT(R, gain, dst=None):
        act(abf[0:R], hb[0:R], AF.Square, accum=col(0)[0:R])
        rstd_of(col(0)[0:R], col(1)[0:R], D)
        stt(abf[0:R], hb[0:R], col(1)[0:R], gain[0:R], ALU.mult, ALU.mult)
        transposes_to(xT if dst is None else dst, abf, 16, R)

    def post_norm(R, gain):
        act(abf[0:R], yacc[0:R], AF.Square, accum=col(2)[0:R])
        rstd_of(col(2)[0:R], col(3)[0:R], D)
        stt(yacc[0:R], yacc[0:R], col(3)[0:R], gain[0:R], ALU.mult, ALU.mult)
        tt(hb[0:R], hb[0:R], yacc[0:R], ALU.add)

    def linear_tok(W, li, ncols, R, src, evac, K=D):
        nk = K // 128
        for g in range(ncols // 512):
            ps = psum()
            for kb in range(0, nk, 8):
                wt = wtile(W[li], kb * 128, g * 512, 8, 512)
                for k in range(8):
                    mm(ps[0:R, :], src[:, kb + k, 0:R], wt[:, k, :], start=(kb + k == 0), stop=(kb + k == nk - 1))
                pump(3)
            evac(g, ps)

    def ln_rows(R, x, n, gb, bb, out, c0):
        act(junk[0:R, 0:n], x, AF.Identity, accum=col(c0)[0:R])
        act(junk[0:R, 0:n], x, AF.Square, accum=col(c0 + 1)[0:R])
        ts(col(c0)[0:R], col(c0)[0:R], 1.0 / n, None, ALU.mult)
        tt(col(c0 + 2)[0:R], col(c0)[0:R], col(c0)[0:R], ALU.mult)
        stt(col(c0 + 1)[0:R], col(c0 + 1)[0:R], 1.0 / n, col(c0 + 2)[0:R], ALU.mult, ALU.subtract)
        rstd_of(col(c0 + 1)[0:R], col(c0 + 1)[0:R], 1.0)
        ts(x, x, col(c0)[0:R], col(c0 + 1)[0:R], ALU.subtract, ALU.mult)
        dma(lnt, gb.to_broadcast([128, 1024]))
        tt(x, x, lnt[0:R], ALU.mult)
        dma(lnt, bb.to_broadcast([128, 1024]))
        tt(out, x, lnt[0:R], ALU.add)

    pending = []
    fgrp = []

    def pump(n=2):
        for _ in range(n):
            while pending:
                try:
                    next(pending[0])
                    break
                except StopIteration:
                    pending.pop(0)
            if not pending:
                return

    def drain():
        while pending:
            pump(1000)

    def ffn_gen(l, grp):
        for part in range(8):
            for g in range(4):
                wt = wtile(w_up[l], 0, (part * 4 + g) * 256, 16, 256)
                if len(grp) == 2:
                    ps = gpsum()
                    for c in range(2):
                        for k in range(16):
                            mm(ps[:, c * 256:(c + 1) * 256], wt[:, k, c * 128:(c + 1) * 128], xTfa[:, k, :], start=(k == 0), stop=(k == 15))
                    pv = ps[:, :].rearrange("p (c r) -> p c r", c=2)
                    jv = junkF[:, :].rearrange("p (c r) -> p c r", c=2)
                    act(jv, pv, AF.Relu)
                    tt(hT[:, g * 2:(g + 1) * 2, :], jv, jv, ALU.mult)
                    yield
                    continue
                for j, (t, R, last) in enumerate(grp):
                    ps = gpsum()
                    for c in range(2):
                        for k in range(16):
                            mm(ps[:, c * 128:c * 128 + R], wt[:, k, c * 128:(c + 1) * 128], xTf[j][:, k, 0:R], start=(k == 0), stop=(k == 15))
                    pv = ps[:, 0:256].rearrange("p (c r) -> p c r", c=2)[:, :, 0:R]
                    jv = junkF[:, 0:256].rearrange("p (c r) -> p c r", c=2)[:, :, 0:R]
                    act(jv, pv, AF.Relu)
                    tt(hT[:, g * 2:(g + 1) * 2, j * 128:j * 128 + R], jv, jv, ALU.mult)
                yield
            for n in range(4):
                wt = wtile(w_down[l], part * 8 * 128, n * 512, 8, 512)
                for j, (t, R, last) in enumerate(grp):
                    ps = gpsum()
                    for k in range(8):
                        mm(ps[0:R, :], hT[:, k, j * 128:j * 128 + R], wt[:, k, :], start=(k == 0), stop=(k == 7))
                    ya = yaccF[j][0:R, n * 512:(n + 1) * 512]
                    if part == 0:
                        cp(ya, ps[0:R, :], eng="act")
                    else:
                        tt(ya, ya, ps[0:R, :], ALU.add)
                yield
        for j, (t, R, last) in enumerate(grp):
            jq = hT[0:R].rearrange("p a b -> p (a b)")
            ya = yaccF[j]
            act(jq, ya[0:R], AF.Square, accum=col(40)[0:R])
            rstd_of(col(40)[0:R], col(41)[0:R], D)
            gF = gain(3, l)
            stt(ya[0:R], ya[0:R], col(41)[0:R], gF[0:R], ALU.mult, ALU.mult)
            if last:
                dst = ys if t == NPT else yp[t * 128:(t + 1) * 128, :]
                key = "yout%d" % t
            else:
                dst = hscr[t, 0:R, :]
                key = "h%d" % t

            def acc_dma(e, dst=dst, src=ya[0:R]):
                return e.dma_start(out=dst, in_=src, accum_op=ALU.add)
            P.op("pool", acc_dma, reads=[ya[0:R]], writes=[key], dma=True)
            yield

    for l in range(DEPTH):
        i = l // 2
        even = (l % 2 == 0)
        curl[0] = l
        if even:
            dma(wpb, w_pool[i].rearrange("g c d -> c g d"), q="pool")
            dma(spc, s_pool[i].rearrange("(g c) -> c g", c=128))
            memset(pext, 0.0)
            memset(S, 0.0); memset(Sb, 0.0)
            dma(o_ps[i].rearrange("(s r) c -> s r c", r=15)[:, 0:11, :], spool[i].rearrange("(s r) c -> s r c", r=15)[:, 4:15, :],
                wk=["o_ps%d" % i])
        else:
            dma(junk[0:31, :], dw_w[i])
            tr32_multi([(junk[0:31, c * 128:(c + 1) * 128], dwT[:, c, :]) for c in range(8)])
            dma(dwbT, dw_b[i].rearrange("(c p) -> p c", p=128))
            dma(sgw, sg_w[i].rearrange("g a b -> a g b"))
            dma(sgbP, sg_b[i].rearrange("g a -> a g"))
            for s in range(16):
                dma(sgbS[4 * s:4 * s + 4, :], sg_b[i, :, 0:4].rearrange("g a -> a g"))
            cp(abf[:, 0:512].rearrange("p (g b) -> p g b", g=4), sgw)
            pst = psum()
            pv4 = pst[:, :].bitcast(BF16).rearrange("p (a b) -> p a b", a=8)
            for g in range(4):
                tr(pv4[:, g, :], abf[:, g * 128:(g + 1) * 128], identb)
            tt(wsT, pv4[:, 0:4, :], T["tri"].rearrange("p (o b) -> p o b", o=1).to_broadcast([128, 4, 128]), ALU.mult)
            for g in range(4):
                for s in range(16):
                    dma(junk[4 * s:4 * s + 4, 0:4], sg_w[i, g, 0:4, 0:4].rearrange("a b -> b a"))
                memset(junk[0:64, 64:128], 0.0)
                for s in range(16):
                    cp(junk[0:64, 64 + 4 * s:64 + 4 * s + 4], junk[0:64, 0:4])
                tt(wsTs[0:64, g, :], junk[0:64, 64:128], T["bd"][0:64], ALU.mult)
            memset(gext, 0.0)
            dma(o_cs[i].rearrange("(s r) c -> s r c", r=30)[:, 0:26, :], sconv[i].rearrange("(s r) c -> s r c", r=30)[:, 4:30, :],
                wk=["o_cs%d" % i])

        for t in range(NPT + 1):
            samp = (t == NPT)
            R = NS if samp else 128
            if l == 0:
                src = xs if samp else xp[t * 128:(t + 1) * 128, :]
                dma(hb[0:R], src)
            else:
                dma(hb[0:R], hscr[t, 0:R, :], rk=["h%d" % t])
            stage("load")
            rms_to_xT(R, gain(0, l))
            stage("rms0")

            if even:
                def evac_even(g, ps):
                    if g == 0:
                        cp(zbuf[0:R, 0:512], ps[0:R, :], eng=os.environ.get("KEV", "act"))
                    elif g <= 3:
                        x = ps[0:R, :].rearrange("p (h d) -> p h d", h=4)
                        cb = cosv[0:R, t:t + 1, :].to_broadcast([R, 4, 64])
                        sb_ = sinv[0:R, t:t + 1, :].to_broadcast([R, 4, 64])
                        o = qk[0:R, (g - 1) * 512:g * 512].rearrange("p (h d) -> p h d", h=4)
                        tt(rt[0][0:R, 0:4], x[:, :, 0:64], cb, ALU.mult)
                        tt(rt[1][0:R, 0:4], x[:, :, 64:128], sb_, ALU.mult)
                        tt(o[:, :, 0:64], rt[0][0:R, 0:4], rt[1][0:R, 0:4], ALU.subtract)
                        tt(rt[2][0:R, 0:4], x[:, :, 0:64], sb_, ALU.mult)
                        tt(rt[3][0:R, 0:4], x[:, :, 64:128], cb, ALU.mult)
                        tt(o[:, :, 64:128], rt[2][0:R, 0:4], rt[3][0:R, 0:4], ALU.add)
                    elif g <= 6:
                        cp(vt[0:R, (g - 4) * 512:(g - 3) * 512], ps[0:R, :], eng="act")
                    else:
                        act(zbuf[0:R, 512 + (g - 7) * 512:512 + (g - 6) * 512], ps[0:R, :], AF.Silu)
                linear_tok(w_in_even, i, 5120, R, xT, evac_even)
                stage("win")
                gs = zbuf[:, 512:2048]
                if samp:
                    for hf in range(2):
                        dma(hloadE[0:120, 0:512], spool[i, hf * 120:(hf + 1) * 120, :])
                        prs = []
                        for c in range(4):
                            prs.append((hloadE[0:120, c * 128:(c + 1) * 128], hstg[:, c, 0:120]))
                        tr32_multi(prs)
                        for c in range(4):
                            cp(pextS[:, c, :].rearrange("p (s r) -> p s r", r=19)[:, hf * 8:(hf + 1) * 8, 0:15],
                               hstg[:, c, 0:120].rearrange("p (s r) -> p s r", r=15))
                if samp:
                    for s in range(16):
                        dma(o_ps[i, s * 15 + 11:s * 15 + 15, :], zbuf[4 * s:4 * s + 4, 0:512], wk=["o_ps%d" % i])
                if samp:
                    tr32_multi([(zbuf[0:R, c * 128:(c + 1) * 128], hstg[:, c, 0:R]) for c in range(4)])
                else:
                    tr32_multi([(zbuf[0:R, c * 128:(c + 1) * 128], pext[:, c, 16:144]) for c in range(4)])
                for c in range(4):
                    w = POOLW[c]
                    if samp:
                        E = pextS[:, c, :]
                        cp(E.rearrange("p (s r) -> p s r", r=19)[:, :, 15:19], hstg[:, c, 0:64].rearrange("p (s r) -> p s r", r=4))
                        L = 304
                    else:
                        E = pext[:, c, :]
                        L = 144
                    a_, b_ = E, pA
                    sh = 1
                    while sh < w:
                        tt(b_[:, sh:L], a_[:, sh:L], a_[:, 0:L - sh], ALU.add)
                        a_ = b_
                        b_ = pB if b_ is pA else pA
                        sh *= 2
                    Ssum = a_
                    if samp:
                        stt(dT[:, c, 0:64].rearrange("p (s r) -> p s r", r=4),
                            Ssum[:, 0:304].rearrange("p (s r) -> p s r", r=19)[:, :, 15:19], 1.0 / w,
                            E.rearrange("p (s r) -> p s r", r=19)[:, :, 15:19], ALU.mult, ALU.subtract)
                    else:
                        stt(dT[:, c, :], Ssum[:, 16:144], 1.0 / w, E[:, 16:144], ALU.mult, ALU.subtract)
                        if t == 0:
                            tt(rt[0][:, 0, 0:16], Ssum[:, 16:32], T["invc"][:, c, :], ALU.mult)
                            tt(dT[:, c, 0:16], rt[0][:, 0, 0:16], E[:, 16:32], ALU.subtract)
                    ps2 = psum()
                    mm(ps2[:, 0:R], wpb[:, c, :], dT[:, c, 0:R])
                    act(yT[:, c, 0:R], ps2[:, 0:R], AF.Copy, scale=spc[:, c:c + 1])
                    if not samp:
                        cp(E[:, 1:16], E[:, 129:144])
                    pump(2)
                if t == NPT - 1:
                    dma(o_pp[i], zbuf[113:128, 0:512], wk=["o_pp%d" % i])
                stage("pool")
                psq = psum(); psk = psum()
                qv = psq[:, :].bitcast(BF16).rearrange("p (a b) -> p a b", a=8)
                kv = psk[:, :].bitcast(BF16).rearrange("p (a b) -> p a b", a=8)
                for h in range(HEADS):
                    tr(qv[:, h, 0:R], qk[0:R, h * 128:(h + 1) * 128], identb[0:R, 0:R])
                    tr(kv[:, h, 0:R], qk[0:R, 768 + h * 128:768 + (h + 1) * 128], identb[0:R, 0:R])
                cp(qT[:, :, 0:R], qv[:, 0:HEADS, 0:R], eng="act")
                tt(qdT[:, :, 0:R], qv[:, 0:HEADS, 0:R], (T["qdecS"] if samp else T["qdecT"])[:, :, 0:R], ALU.mult)
                cp(kT[:, :, 0:R], kv[:, 0:HEADS, 0:R], eng="act")
                kdt = T["kdecS"] if samp else T["kdec"]
                tt(kd[0:R], qk[0:R, 768:1536].rearrange("p (h d) -> p h d", h=HEADS),
                   kdt[0:R].rearrange("p (h o) -> p h o", o=1).to_broadcast([R, HEADS, 128]), ALU.mult)
                psc = [psum(), psum()]
                for h in range(HEADS):
                    mm(psc[h // 4][0:R, (h % 4) * 128:(h % 4) * 128 + R], kT[:, h, 0:R], qT[:, h, 0:R])
                mk = maskSb if samp else maskTb
                tt(PT[0:R, 0:4, 0:R], psc[0][0:R, :].rearrange("p (h l) -> p h l", h=4)[:, :, 0:R], mk[0:R, 0:4, 0:R], ALU.mult)
                tt(PT[0:R, 4:6, 0:R], psc[1][0:R, 0:256].rearrange("p (h l) -> p h l", h=2)[:, :, 0:R], mk[0:R, 4:6, 0:R], ALU.mult)
                if not samp:
                    pso = [psum(), psum(), psum()]

                    def oap(h):
                        return pso[h // 2][0:R, (h % 2) * 256:(h % 2 + 1) * 256]
                    for h in range(HEADS):
                        mm(oap(h), PT[0:R, h, 0:R], vt[0:R, h * 256:(h + 1) * 256], start=True, stop=False)
                        mm(oap(h), qdT[:, h, 0:R], Sb[:, h, :], start=False, stop=True)
                    pss = [psum(), psum(), psum()]
                    for h in range(HEADS):
                        sp_ = pss[h // 2][:, (h % 2) * 256:(h % 2 + 1) * 256]
                        mm(sp_, kd[0:R, h, :], vt[0:R, h * 256:(h + 1) * 256])
                        stt(S[:, h, :], S[:, h, :], float(cd[h]), sp_, ALU.mult, ALU.add)
                        cp(Sb[:, h, :], S[:, h, :], eng="act")
                    if t == NPT - 1:
                        dma(o_rp[i].rearrange("h d e -> d h e"), S, wk=["o_rp%d" % i])
                else:
                    oacc = yacc[0:64, 0:1536]
                    pso = [psum(), psum(), psum()]
                    for h in range(HEADS):
                        oh = pso[h // 2][0:R, (h % 2) * 256:(h % 2 + 1) * 256]
                        mm(oh, PT[0:R, h, 0:R], vt[0:R, h * 256:(h + 1) * 256])
                    for hp in range(3):
                        cp(oacc[:, hp * 512:(hp + 1) * 512], pso[hp][0:R, :], eng="act")
                    for s in range(16):
                        dma(S0, sret[i, s].rearrange("h d e -> d h e"))
                        cp(S0b, S0, eng="act")
                        tt(qdm, qdT[:, :, 0:64], T["colmask"][:, s:s + 1, :].to_broadcast([128, HEADS, 64]), ALU.mult)
                        ts(kdm[0:64], kd[0:64].rearrange("p h d -> p (h d)"), T["onehot"][0:64, s:s + 1], None, ALU.mult)
                        pso = [psum(), psum(), psum()]
                        for h in range(HEADS):
                            oh = pso[h // 2][0:R, (h % 2) * 256:(h % 2 + 1) * 256]
                            mm(oh, qdm[:, h, :], S0b[:, h, :])
                        for hp in range(3):
                            tt(oacc[:, hp * 512:(hp + 1) * 512], oacc[:, hp * 512:(hp + 1) * 512], pso[hp][0:R, :], ALU.add)
                        pss = [psum(), psum(), psum()]
                        for h in range(HEADS):
                            sp_ = pss[h // 2][:, (h % 2) * 256:(h % 2 + 1) * 256]
                            mm(sp_, kdm[0:64, h * 128:(h + 1) * 128], vt[0:64, h * 256:(h + 1) * 256])
                            stt(S0[:, h, :], S0[:, h, :], float(cd4[h]), sp_, ALU.mult, ALU.add)
                        dma(o_rs[i, s].rearrange("h d e -> d h e"), S0, wk=["o_rs%d_%d" % (i, s)])
                        pump(3)

                    def oap(h):
                        return oacc[:, h * 256:(h + 1) * 256]
                stage("ret")
                for h in range(HEADS):
                    act(junk[0:R, 0:256], oap(h), AF.Square, accum=col(8 + h)[0:R])
                act(small[0:R, 16:22], small[0:R, 8:14], AF.Sqrt, bias=EPS, scale=1.0 / 256)
                recip(small[0:R, 16:22], small[0:R, 16:22])
                for h in range(HEADS):
                    stt(yin[0:R, h * 256:(h + 1) * 256], oap(h), col(16 + h)[0:R], gs[0:R, h * 256:(h + 1) * 256], ALU.mult, ALU.mult)
                transposes_to(yT, yin, 12, R, dst_chunk0=4)
                Wout = w_out_even
            else:
                def evac_odd(g, ps):
                    if g < 4:
                        act(zbuf[0:R, g * 512:(g + 1) * 512], ps[0:R, :], AF.Gelu_apprx_tanh)
                    elif g < 6:
                        cp(zbuf[0:R, g * 512:(g + 1) * 512], ps[0:R, :], eng="act")
                    else:
                        act(junk[0:R, 0:512], ps[0:R, :], AF.Sigmoid)
                        tt(zbuf[0:R, (g - 2) * 512:(g - 1) * 512], zbuf[0:R, (g - 2) * 512:(g - 1) * 512], junk[0:R, 0:512], ALU.mult)
                linear_tok(w_in_odd, i, 4096, R, xT, evac_odd)
                ug = zbuf[:, 0:1024]; vv = zbuf[:, 1024:2048]; glu = zbuf[:, 2048:3072]
                ln_rows(R, vv[0:R], 1024, sg_ln_g[i:i + 1, :], sg_ln_b[i:i + 1, :], vv[0:R], 24)
                if samp:
                    dma(o_sv[i], vv[0:64], wk=["o_sv%d" % i])
                    for s in range(16):
                        dma(o_cs[i, s * 30 + 26:s * 30 + 30, :], glu[4 * s:4 * s + 4], wk=["o_cs%d" % i])
                cp(abf[0:R, 0:1024], vv[0:R], eng="act")
                for g in range(4):
                    ps = psum()
                    if samp:
                        mm(ps[0:64, 0:256], wsTs[0:64, g, :], abf[0:64, g * 256:(g + 1) * 256])
                        bcol = sgbS[0:64, g:g + 1]
                    else:
                        mm(ps[:, 0:256], wsT[:, g, :], abf[:, g * 256:(g + 1) * 256])
                        bcol = sgbP[:, g:g + 1]
                    stt(yin[0:R, g * 256:(g + 1) * 256], ps[0:R, 0:256], bcol, ug[0:R, g * 256:(g + 1) * 256], ALU.add, ALU.mult)
                if samp:
                    for q4 in range(4):
                        dma(hloadO[0:120, :], sconv[i, q4 * 120:(q4 + 1) * 120, :])
                        tr32_multi([(hloadO[0:120, c * 128:(c + 1) * 128], hstg[:, c, 0:120]) for c in range(8)])
                        for c in range(8):
                            cp(gextS[:, c, :].rearrange("p (s r) -> p s r", r=34)[:, q4 * 4:(q4 + 1) * 4, 0:30],
                               hstg[:, c, 0:120].rearrange("p (s r) -> p s r", r=30))
                if samp:
                    tr32_multi([(glu[0:R, c * 128:(c + 1) * 128], hstg[:, c, 0:R]) for c in range(8)])
                else:
                    tr32_multi([(glu[0:R, c * 128:(c + 1) * 128], gext[:, c, 32:160]) for c in range(8)])
                for c in range(8):
                    if samp:
                        E3 = gextS[:, c, :].rearrange("p (s r) -> p s r", r=34)
                        cp(E3[:, :, 30:34], hstg[:, c, 0:64].rearrange("p (s r) -> p s r", r=4))
                        acc = cacc[:, c, 0:64].rearrange("p (s r) -> p s r", r=4)
                        ts(acc, E3[:, :, 0:4], dwT[:, c, 0:1], dwbT[:, c:c + 1], ALU.mult, ALU.add)
                        for j in range(1, 31):
                            stt(acc, E3[:, :, j:j + 4], dwT[:, c, j:j + 1], acc, ALU.mult, ALU.add)
                    else:
                        E = gext[:, c, :]
                        acc = cacc[:, c, :]
                        ts(acc, E[:, 2:130], dwT[:, c, 0:1], dwbT[:, c:c + 1], ALU.mult, ALU.add)
                        for j in range(1, 31):
                            stt(acc, E[:, 2 + j:130 + j], dwT[:, c, j:j + 1], acc, ALU.mult, ALU.add)
                        cp(E[:, 2:32], E[:, 130:160])
                    pump(3)
                if t == NPT - 1:
                    dma(o_cp[i], glu[98:128], wk=["o_cp%d" % i])
                tr32_multi([(cacc[:, c, 0:R], zbuf[0:R, 3072 + c * 128:3072 + (c + 1) * 128]) for c in range(8)])
                cv = zbuf[:, 3072:4096]
                ln_rows(R, cv[0:R], 1024, cv_ln_g[i:i + 1, :], cv_ln_b[i:i + 1, :], cv[0:R], 28)
                act(yin[0:R, 1024:2048], cv[0:R], AF.Silu)
                transposes_to(yT, yin, 16, R)
                Wout = w_out_odd

            stage("mix")
            def evac_y(g, ps):
                cp(yacc[0:R, g * 512:(g + 1) * 512], ps[0:R, :], eng="act")
            linear_tok(Wout, i, D, R, yT, evac_y)
            drain()
            post_norm(R, gain(1, l))
            stage("wout")
            last = (l == DEPTH - 1)
            if last:
                dma(ys if samp else yp[t * 128:(t + 1) * 128, :], hb[0:R], wk=["yout%d" % t])
            else:
                dma(hscr[t, 0:R, :], hb[0:R], wk=["h%d" % t])
            rms_to_xT(R, gain(2, l), dst=xTf[len(fgrp)])
            fgrp.append((t, R, last))
            if len(fgrp) == 2 or samp:
                pending.append(ffn_gen(l, list(fgrp)))
                del fgrp[:]
            stage("ffn")
    drain()

    for q in ("sp", "pool"):
        n = P.ndma[q]
        toks = [("d", q, k) for k in range(max(0, n - RING[q]), n)]
        waits = [tk for tk in toks if tk not in P.dknown["sp"]]
        P.ops["sp"].append((waits, None, ("c", "sp", len(P.ops["sp"]))))

    with nc.Block() as block:
        P.emit(nc, block, sems, dsems)
    es.close()
    return nc


_CACHE = {}


def kernel(**inp):
    tabs, cd, cd4 = host_tables()
    if "nc" not in _CACHE:
        _CACHE["nc"] = build(cd, cd4)
    nc = _CACHE["nc"]
    f = lambda a: np.ascontiguousarray(np.asarray(a, dtype=np.float32))
    norms = f(np.concatenate([inp["norm_mix_pre"], inp["norm_mix_post"], inp["norm_ffn_pre"], inp["norm_ffn_post"]], 0))
    shared = {k: f(inp[k]) for k in ("w_in_even", "w_pool", "s_pool", "w_out_even", "w_in_odd", "sg_ln_g", "sg_ln_b", "sg_w",
                                     "sg_b", "dw_w", "dw_b", "cv_ln_g", "cv_ln_b", "w_out_odd", "w_up", "w_down")}
    shared["norms"] = norms
    for k, v in tabs.items():
        shared["t_" + k] = f(v)
    xpr = f(inp["x_prompt"]); xsm = f(inp["x_sample"])
    sp_ = f(inp["state_pool"]); sr_ = f(inp["state_ret"]); sc_ = f(inp["state_conv"])
    in_maps = []
    for c in range(8):
        m = dict(shared)
        m["xp"] = xpr[c % 4]
        m["xs"] = xsm[16 * c:16 * c + 16].reshape(NS, D)
        m["spool"] = np.ascontiguousarray(sp_[:, 16 * c:16 * c + 16].reshape(2, 240, 512))
        m["sret"] = np.ascontiguousarray(sr_[:, 16 * c:16 * c + 16])
        m["sconv"] = np.ascontiguousarray(sc_[:, 16 * c:16 * c + 16].reshape(2, 480, 1024))
        in_maps.append(m)
    res = run_bass_kernel_spmd(nc, in_maps, core_ids=list(range(8)))
    R_ = res.results
    y_prompt = np.stack([R_[b]["yp"] for b in range(4)], 0)
    y_sample = np.concatenate([R_[c]["ys"].reshape(16, 4, D) for c in range(8)], 0)
    pool_p = np.stack([R_[b]["o_pp"] for b in range(4)], 1)
    pool_s = np.concatenate([R_[c]["o_ps"].reshape(2, 16, 15, 512) for c in range(8)], 1)
    ret_p = np.stack([R_[b]["o_rp"] for b in range(4)], 1)
    ret_s = np.concatenate([R_[c]["o_rs"] for c in range(8)], 1)
    conv_p = np.stack([R_[b]["o_cp"] for b in range(4)], 1)
    conv_s = np.concatenate([R_[c]["o_cs"].reshape(2, 16, 30, 1024) for c in range(8)], 1)
    sgv_s = np.concatenate([R_[c]["o_sv"].reshape(2, 16, 4, 1024) for c in range(8)], 1)
    outs = (y_prompt, y_sample, pool_p, pool_s, ret_p, ret_s, conv_p, conv_s, sgv_s)
    return tuple(np.ascontiguousarray(o, dtype=np.float32) for o in outs)
```

```python
import numpy as np
import concourse.bass as bass
import concourse.mybir as mybir
from concourse.bass_utils import run_bass_kernel_spmd

F32 = mybir.dt.float32
BF16 = mybir.dt.bfloat16
AF = mybir.ActivationFunctionType
ALU = mybir.AluOpType

D = 2048
DEPTH = 4
SEQ = 2048
NPT = SEQ // 128
NS = 64
HEADS = 6
EPS = 1e-6
POOLW = (2, 4, 8, 16)
ENGS = ("pe", "act", "dve", "pool", "sp")
RING = {"sp": 12, "pool": 6}
BLK = 256


class Prog:
    def __init__(self):
        self.ops = {e: [] for e in ENGS}
        self.clock = {e: {f: -1 for f in ENGS} for e in ENGS}
        self.dknown = {e: set() for e in ENGS}
        self.snap = {}
        self.state = {}
        self.ndma = {"sp": 0, "pool": 0}
        self.signal = set()
        self.cache = {}

    def keys(self, x):
        if isinstance(x, str):
            return (x,)
        name = x.tensor.name
        if name != "SB":
            return (name,)
        esz = 4 if x.dtype == F32 else 2
        ck = (x.offset, x.ap, esz)
        r = self.cache.get(ck)
        if r is not None:
            return r
        ap = x.ap
        pstep = ap[0][0]
        off = x.offset % pstep
        dims = [(s, c) for (s, c) in ap[1:] if c > 1]
        if not dims:
            ivs = [(off, off + 1)]
        else:
            inner = dims[-1]
            outer = dims[:-1]
            ncomb = 1
            for s, c in outer:
                ncomb *= c
            if inner[0] == 1 and ncomb <= 512:
                starts = [off]
                for s, c in outer:
                    starts = [b + s * i for b in starts for i in range(c)]
                ivs = [(b, b + inner[1]) for b in starts]
            else:
                hi = off + sum(s * (c - 1) for s, c in dims) + 1
                ivs = [(off, hi)]
        ks = set()
        for lo, hi in ivs:
            for b in range((lo * esz) // BLK, ((hi * esz) - 1) // BLK + 1):
                ks.add(b)
        r = tuple(ks)
        self.cache[ck] = r
        return r

    dead = False
    maxops = 0

    def op(self, eng, fn, reads=(), writes=(), dma=False):
        if self.dead:
            return None
        self.nops = getattr(self, "nops", 0) + 1
        if self.maxops and self.nops > self.maxops:
            self.dead = True
            return None
        deps = set()
        rk = [k for x in reads for k in self.keys(x)]
        wk = [k for x in writes for k in self.keys(x)]
        psr = [k for k in rk if isinstance(k, str) and k.startswith("PS")]
        if psr:
            rk = [k for k in rk if not (isinstance(k, str) and k.startswith("PS"))]
            wk = wk + psr
        for k in rk:
            st = self.state.get(k)
            if st and st[0] is not None:
                deps.add(st[0])
        for k in wk:
            st = self.state.get(k)
            if st:
                if st[0] is not None:
                    deps.add(st[0])
                deps.update(st[1].values())
        idx = len(self.ops[eng])
        if dma:
            n = self.ndma[eng]
            self.ndma[eng] = n + 1
            tok = ("d", eng, n)
            if n >= RING[eng]:
                deps.add(("d", eng, n - RING[eng]))
        else:
            tok = ("c", eng, idx)
        clk = self.clock[eng]
        best = {}
        ddeps = []
        for dtok in deps:
            if dtok[0] == "c":
                _, f, i = dtok
                if f == eng and eng in ("pe", "sp"):
                    continue
                if clk[f] >= i:
                    continue
                if f not in best or best[f] < i:
                    best[f] = i
            else:
                if dtok not in self.dknown[eng]:
                    ddeps.append(dtok)
        waits = [("c", f, i) for f, i in best.items()] + ddeps
        for dtok in waits:
            if dtok[0] == "c":
                self.signal.add(dtok)
                sn = self.snap[dtok]
                for f in ENGS:
                    if sn[f] > clk[f]:
                        clk[f] = sn[f]
                if dtok[2] > clk[dtok[1]]:
                    clk[dtok[1]] = dtok[2]
            else:
                self.dknown[eng].add(dtok)
                sn = self.snap[dtok]
                for f in ENGS:
                    if sn[f] > clk[f]:
                        clk[f] = sn[f]
        sn = dict(clk)
        self.snap[tok] = sn
        self.ops[eng].append((waits, fn, tok))
        for k in rk:
            st = self.state.setdefault(k, [None, {}])
            if tok[0] == "c":
                st[1][(tok[0], tok[1])] = tok
            else:
                st[1][tok] = tok
        for k in wk:
            self.state[k] = [tok, {}]
        return tok

    def emit(self, nc, block, sems, dsems):
        semval = {}
        for e in ENGS:
            cnt = 0
            for i, (_, _, tok) in enumerate(self.ops[e]):
                if tok in self.signal:
                    cnt += 1
                    semval[tok] = cnt

        def run(e, engobj):
            for waits, fn, tok in self.ops[e]:
                for w in waits:
                    if w[0] == "c":
                        engobj.wait_ge(sems[w[1]], semval[w])
                    else:
                        r = RING[w[1]]
                        engobj.wait_ge(dsems[w[1]][w[2] % r], 16 * (w[2] // r + 1))
                if fn is None:
                    continue
                ins = fn(engobj)
                if tok[0] == "d":
                    ins.then_inc(dsems[tok[1]][tok[2] % RING[tok[1]]], 16)
                elif tok in self.signal:
                    ins.then_inc(sems[e], 1)

        block.tensor(lambda g: run("pe", g))
        block.scalar(lambda g: run("act", g))
        block.vector(lambda g: run("dve", g))
        block.gpsimd(lambda g: run("pool", g))
        block.sync(lambda g: run("sp", g))


def host_tables(NPT=NPT):
    SEQ = NPT * 128
    half = 64
    inv = (10000.0 ** (-np.arange(half, dtype=np.float32) / np.float32(half))).astype(np.float32)
    pos_p = np.arange(SEQ, dtype=np.float32)
    pos_s = (16384 + np.arange(4)).astype(np.float32)
    pos_s = np.tile(pos_s, 16)
    ang_p = (pos_p[:, None] * inv[None, :]).astype(np.float32)
    ang_s = (pos_s[:, None] * inv[None, :]).astype(np.float32)
    cosT = np.zeros((128, NPT + 1, 64), np.float32)
    sinT = np.zeros((128, NPT + 1, 64), np.float32)
    cosT[:, :NPT] = np.cos(ang_p).astype(np.float32).reshape(NPT, 128, 64).transpose(1, 0, 2)
    sinT[:, :NPT] = np.sin(ang_p).astype(np.float32).reshape(NPT, 128, 64).transpose(1, 0, 2)
    cosT[:64, NPT] = np.cos(ang_s).astype(np.float32)
    sinT[:64, NPT] = np.sin(ang_s).astype(np.float32)
    hh = np.arange(HEADS, dtype=np.float32)
    log_g = np.log1p(-np.exp2(-5.0 - hh)).astype(np.float32)
    scale = np.float32(128 ** -0.5)
    idx = np.arange(128, dtype=np.float32)
    diff = idx[:, None] - idx[None, :]
    mask = np.where(diff[None] >= 0, np.exp(log_g[:, None, None] * np.maximum(diff, 0.0)[None]), 0.0)
    maskT = (mask.transpose(2, 0, 1) * scale).astype(np.float32)
    i4 = np.arange(4, dtype=np.float32)
    d4 = i4[:, None] - i4[None, :]
    m4 = np.where(d4[None] >= 0, np.exp(log_g[:, None, None] * np.maximum(d4, 0.0)[None]), 0.0)
    maskS = np.zeros((64, HEADS, 64), np.float32)
    for s in range(16):
        maskS[4 * s:4 * s + 4, :, 4 * s:4 * s + 4] = m4.transpose(2, 0, 1) * scale
    qdec = np.exp(log_g[:, None] * (idx + 1.0)[None, :]).astype(np.float32)
    qdecT = np.broadcast_to(qdec[None], (128, HEADS, 128)).copy()
    qdec4 = np.exp(log_g[:, None] * (i4 + 1.0)[None, :]).astype(np.float32)
    qdecS = np.broadcast_to(np.tile(qdec4, (1, 16))[None], (128, HEADS, 64)).copy()
    kdec = (np.exp(log_g[None, :] * (127.0 - idx)[:, None]) * scale).astype(np.float32)
    kdec4 = (np.exp(log_g[None, :] * (3.0 - i4)[:, None]) * scale).astype(np.float32)
    kdecS = np.zeros((128, HEADS), np.float32)
    kdecS[:64] = np.tile(kdec4, (16, 1))
    colmask = np.zeros((128, 16, 64), np.float32)
    onehot = np.zeros((128, 16), np.float32)
    for s in range(16):
        colmask[:, s, 4 * s:4 * s + 4] = 1.0
        onehot[4 * s:4 * s + 4, s] = 1.0
    invc = np.zeros((128, 4, 16), np.float32)
    for g, w in enumerate(POOLW):
        invc[:, g, :] = 1.0 / np.minimum(w, np.arange(16) + 1.0)
    tri = np.tril(np.ones((128, 128), np.float32)).T.copy()
    bd = np.zeros((128, 64), np.float32)
    for s in range(16):
        for j in range(4):
            for i in range(j, 4):
                bd[4 * s + j, 4 * s + i] = 1.0
    cd = np.exp(log_g * 128.0).astype(np.float32)
    cd4 = np.exp(log_g * 4.0).astype(np.float32)
    return dict(cosT=cosT, sinT=sinT, maskT=maskT, maskS=np.concatenate([maskS, np.zeros_like(maskS)], 0),
                qdecT=qdecT, qdecS=qdecS, kdec=kdec, kdecS=kdecS, colmask=colmask, onehot=onehot,
                invc=invc, tri=tri, bd=bd), cd, cd4


def table_shapes(NPT):
    return dict(cosT=[128, NPT + 1, 64], sinT=[128, NPT + 1, 64], maskT=[128, HEADS, 128], maskS=[128, HEADS, 64],
                qdecT=[128, HEADS, 128], qdecS=[128, HEADS, 64], kdec=[128, HEADS], kdecS=[128, HEADS],
                    colmask=[128, 16, 64], onehot=[128, 16], invc=[128, 4, 16], tri=[128, 128], bd=[128, 64])


def build(cd, cd4, DEPTH=DEPTH, NPT=NPT):
    SEQ = NPT * 128
    NE = (DEPTH + 1) // 2
    NO = max(DEPTH // 2, 1)
    nc = bass.Bass("TRN2", target_bir_lowering=False)
    P = Prog()

    def din(name, shape):
        return nc.dram_tensor(name, list(shape), F32, kind="ExternalInput").ap()

    def dout(name, shape):
        return nc.dram_tensor(name, list(shape), F32, kind="ExternalOutput").ap()

    xp = din("xp", [SEQ, D]); xs = din("xs", [NS, D])
    spool = din("spool", [NE, 240, 512]); sret = din("sret", [NE, 16, HEADS, 128, 256]); sconv = din("sconv", [NO, 480, 1024])
    norms = din("norms", [4 * DEPTH, D])
    w_in_even = din("w_in_even", [NE, D, 5120]); w_pool = din("w_pool", [NE, 4, 128, 128]); s_pool = din("s_pool", [NE, 512])
    w_out_even = din("w_out_even", [NE, D, D]); w_in_odd = din("w_in_odd", [NO, D, 4096])
    sg_ln_g = din("sg_ln_g", [NO, 1024]); sg_ln_b = din("sg_ln_b", [NO, 1024]); sg_w = din("sg_w", [NO, 4, 128, 128]); sg_b = din("sg_b", [NO, 4, 128])
    dw_w = din("dw_w", [NO, 31, 1024]); dw_b = din("dw_b", [NO, 1024]); cv_ln_g = din("cv_ln_g", [NO, 1024]); cv_ln_b = din("cv_ln_b", [NO, 1024])
    w_out_odd = din("w_out_odd", [NO, D, D]); w_up = din("w_up", [DEPTH, D, 8192]); w_down = din("w_down", [DEPTH, 8192, D])
    TABLE_SHAPES = table_shapes(NPT)
    tabs = {k: din("t_" + k, v) for k, v in TABLE_SHAPES.items()}
    yp = dout("yp", [SEQ, D]); ys = dout("ys", [NS, D])
    o_pp = dout("o_pp", [NE, 15, 512]); o_ps = dout("o_ps", [NE, 240, 512])
    o_rp = dout("o_rp", [NE, HEADS, 128, 256]); o_rs = dout("o_rs", [NE, 16, HEADS, 128, 256])
    o_cp = dout("o_cp", [NO, 30, 1024]); o_cs = dout("o_cs", [NO, 480, 1024]); o_sv = dout("o_sv", [NO, NS, 1024])
    hscr = nc.dram_tensor("hscr", [NPT + 1, 128, D], F32).ap()

    SBW = 103 * 1024
    import contextlib
    es = contextlib.ExitStack()
    SB = es.enter_context(nc.sbuf_tensor("SB", [128, SBW], BF16))
    banks = [es.enter_context(nc.psum_tensor("PS%d" % i, [128, 512], F32)) for i in range(8)]
    sems = {e: es.enter_context(nc.semaphore("s_" + e)) for e in ENGS}
    dsems = {q: [es.enter_context(nc.semaphore("d_%s%d" % (q, i))) for i in range(RING[q])] for q in RING}
    nc.allow_low_precision("bf16 matmul operands, fp32 accumulation")
    es.enter_context(nc.allow_non_contiguous_dma("small strided parameter loads"))

    import os
    kstop = os.environ.get("KSTOP", "")

    P.maxops = int(os.environ.get("KMAXOPS", "0"))

    def stage(name):
        if os.environ.get("KVERBOSE"):
            print("stage", name, "nops", getattr(P, "nops", 0))
        if kstop and name == kstop:
            P.dead = True

    cur = [0]

    def alloc(dt, *dims):
        n = 1
        for d_ in dims:
            n *= d_
        nb = n * (4 if dt == F32 else 2)
        nb = (nb + BLK - 1) // BLK * BLK
        off = cur[0]
        cur[0] += nb
        assert cur[0] <= SBW * 2, "SBUF overflow"
        v = SB[:, off // 2:(off + nb) // 2]
        if dt == F32:
            v = v.bitcast(F32)
        v = v[:, 0:n]
        if len(dims) == 2:
            v = v.rearrange("p (a b) -> p a b", a=dims[0])
        elif len(dims) == 3:
            v = v.rearrange("p (a b c) -> p a b c", a=dims[0], b=dims[1])
        return v

    pb = [0]

    held = set()

    def psum():
        k = pb[0] % 6
        pb[0] += 1
        return banks[k]

    gpb = [0]

    def gpsum():
        k = 6 + gpb[0] % 2
        gpb[0] += 1
        return banks[k]

    def hold(n):
        out = []
        for _ in range(n):
            while pb[0] % 8 in held:
                pb[0] += 1
            k = pb[0] % 8
            pb[0] += 1
            held.add(k)
            out.append(banks[k])
        return out

    def release(bs):
        for b in bs:
            held.discard(banks.index(b))

    def mm(out, lhsT, rhs, start=True, stop=True):
        P.op("pe", lambda e: e.matmul(out, lhsT, rhs, start=start, stop=stop), reads=[lhsT, rhs], writes=[out])

    def tr(out, in_, ident):
        P.op("pe", lambda e: e.transpose(out, in_, ident), reads=[in_, ident], writes=[out])

    def act(out, in_, func, bias=None, scale=None, accum=None):
        kw = {}
        rd = [in_]
        if bias is not None:
            kw["bias"] = bias
            if not isinstance(bias, float):
                rd.append(bias)
        if scale is not None:
            kw["scale"] = scale
            if not isinstance(scale, float):
                rd.append(scale)
        wr = [out]
        if accum is not None:
            kw["accum_out"] = accum
            wr.append(accum)
            P.op("dve", lambda e: e.memset(accum, 0.0), writes=[accum])
        P.op("act", lambda e: e.activation(out, in_, func, **kw), reads=rd, writes=wr)

    def tt(out, in0, in1, op, eng="dve"):
        P.op(eng, lambda e: e.tensor_tensor(out, in0, in1, op), reads=[in0, in1], writes=[out])

    def ts(out, in0, s1, s2, op0, op1=None, eng="dve"):
        rd = [in0] + [s for s in (s1, s2) if s is not None and not isinstance(s, float)]
        if op1 is None:
            P.op(eng, lambda e: e.tensor_scalar(out, in0, s1, None, op0), reads=rd, writes=[out])
        else:
            P.op(eng, lambda e: e.tensor_scalar(out, in0, s1, s2, op0, op1), reads=rd, writes=[out])

    def stt(out, in0, scalar, in1, op0, op1, eng="dve"):
        rd = [in0, in1] + ([] if isinstance(scalar, float) else [scalar])
        P.op(eng, lambda e: e.scalar_tensor_tensor(out, in0, scalar, in1, op0, op1), reads=rd, writes=[out])

    def cp(out, in_, eng="dve"):
        if eng == "act":
            P.op("act", lambda e: e.activation(out, in_, AF.Identity), reads=[in_], writes=[out])
        else:
            P.op(eng, lambda e: e.tensor_copy(out, in_), reads=[in_], writes=[out])

    def recip(out, in_):
        P.op("dve", lambda e: e.reciprocal(out, in_), reads=[in_], writes=[out])

    def memset(ap, val, eng="dve"):
        P.op(eng, lambda e: e.memset(ap, val), writes=[ap])

    def dma(out, in_, q="sp", rk=(), wk=()):
        rd = list(rk) + ([in_] if in_.tensor.name == "SB" else [])
        wr = list(wk) + ([out] if out.tensor.name == "SB" else [])
        return P.op(q, lambda e: e.dma_start(out=out, in_=in_), reads=rd, writes=wr, dma=True)

    for z0 in range(0, SBW, 16384):
        z1 = min(SBW, z0 + 16384)
        memset(SB[:, z0:z1], 0.0, eng="pool" if (z0 // 16384) % 2 else "dve")
    identf = alloc(F32, 128)
    identb = alloc(BF16, 128)
    P.op("pool", lambda e: e.memset(identf, 0.0), writes=[identf])
    P.op("pool", lambda e: e.affine_select(out=identf, in_=identf, compare_op=ALU.not_equal, fill=1.0, base=0,
                                           pattern=[[-1, 128]], channel_multiplier=1), reads=[identf], writes=[identf])
    cp(identb, identf)
    T = {}
    for k, shp in TABLE_SHAPES.items():
        T[k] = alloc(F32, *shp[1:])
        dma(T[k], tabs[k])
    maskTb = alloc(BF16, HEADS, 128); cp(maskTb, T["maskT"])
    maskSb = alloc(BF16, HEADS, 64); cp(maskSb, T["maskS"])
    cosv = T["cosT"]; sinv = T["sinT"]

    stage("init")
    hb = alloc(F32, D); yacc = alloc(F32, D); abf = alloc(BF16, D); junk = alloc(F32, 1024)
    xT = alloc(BF16, 16, 128); yT = alloc(BF16, 16, 128); yin = alloc(BF16, D)
    gbc = [alloc(F32, D)]
    xTfa = alloc(BF16, 16, 384); xTf = [xTfa[:, :, q * 128:(q + 1) * 128] for q in range(3)]
    yaccF = [alloc(F32, D) for _ in range(2)]; junkF = alloc(F32, 512); hT = alloc(BF16, 8, 256)
    gn = [0]
    WR = [alloc(BF16, 4096) for _ in range(4)]
    wn = [0]
    small = alloc(F32, 64)
    zbuf = alloc(F32, 4096)
    rt = [alloc(F32, 6, 64) for _ in range(4)]
    hstg = alloc(F32, 8, 128)
    base1 = cur[0]
    S = alloc(F32, HEADS, 256); Sb = alloc(BF16, HEADS, 256)
    pext = alloc(F32, 4, 144); wpb = alloc(BF16, 4, 128); spc = alloc(F32, 4)
    e1 = cur[0]
    cur[0] = base1
    gext = alloc(F32, 8, 160); dwT = alloc(F32, 8, 31); dwbT = alloc(F32, 8)
    lnt = alloc(F32, 1024)
    wsT = alloc(BF16, 4, 128); wsTs = alloc(BF16, 4, 64); sgw = alloc(F32, 4, 128); sgbP = alloc(F32, 4); sgbS = alloc(F32, 4)
    base2 = max(e1, cur[0])
    cur[0] = base2
    qk = alloc(BF16, 1536); vt = alloc(BF16, 1536)
    qT = alloc(BF16, HEADS, 128); qdT = alloc(BF16, HEADS, 128); kT = alloc(BF16, HEADS, 128); kd = alloc(BF16, HEADS, 128)
    PT = alloc(BF16, HEADS, 128)
    S0 = zbuf[:, 2048:3584].rearrange("p (h e) -> p h e", h=HEADS); S0b = alloc(BF16, HEADS, 256); qdm = alloc(BF16, HEADS, 64); kdm = alloc(BF16, HEADS * 128)
    pextS = alloc(F32, 4, 304); pA = alloc(F32, 304); pB = alloc(F32, 304)
    dT = alloc(BF16, 4, 128); hloadE = alloc(F32, 512)
    e2 = cur[0]
    cur[0] = base2
    gextS = alloc(F32, 8, 16 * 34); cacc = alloc(F32, 8, 128); hloadO = alloc(F32, 1024)
    e3 = cur[0]
    cur[0] = base2
    pass
    cur[0] = max(e2, e3, cur[0])

    print("SBUF bytes used", cur[0], "of", SBW * 2)

    def gain(j, l_):
        b = gbc[0]
        gn[0] += 1
        r = DEPTH * j + l_
        dma(b, norms[r:r + 1, :].to_broadcast([128, D]))
        return b

    def col(i):
        return small[:, i:i + 1]

    wscr = [nc.dram_tensor("wscr%d" % l_, [96, 128, 4096], BF16).ap() for l_ in range(DEPTH)]
    wids = {}
    wcnt = [0] * DEPTH
    curl = [0]

    def wtile(W, r0, c0, nk, ncols):
        b = WR[wn[0] % len(WR)]
        wn[0] += 1
        bv = b.rearrange("p (k n) -> p k n", k=nk)
        key = (W.tensor.name, W.offset, r0, c0, nk)
        if key not in wids:
            wid = (curl[0], wcnt[curl[0]])
            wcnt[curl[0]] += 1
            wids[key] = wid
            dma(bv, W[r0:r0 + nk * 128, c0:c0 + ncols].rearrange("(k p) n -> p k n", p=128), q="pool")
            dma(wscr[wid[0]][wid[1]], b, q="sp", wk=["wscr%d_%d" % wid])
        else:
            wid = wids[key]
            dma(b, wscr[wid[0]][wid[1]], q="pool", rk=["wscr%d_%d" % wid])
        return bv

    def rstd_of(ss_col, out_col, n):
        act(out_col, ss_col, AF.Sqrt, bias=EPS, scale=1.0 / n)
        recip(out_col, out_col)

    def transposes_to(dst, src_bf, nchunks, R, dst_chunk0=0):
        c = 0
        while c < nchunks:
            n = min(8, nchunks - c)
            pst = psum()
            psv = pst[:, :].bitcast(BF16).rearrange("p (a b) -> p a b", a=8)
            for j in range(n):
                tr(psv[:, j, 0:R], src_bf[0:R, (c + j) * 128:(c + j + 1) * 128], identb[0:R, 0:R])
            cp(dst[:, dst_chunk0 + c:dst_chunk0 + c + n, 0:R], psv[:, 0:n, 0:R], eng="act")
            c += n

    def tr32_multi(pairs):
        for b0 in range(0, len(pairs), 4):
            grp = pairs[b0:b0 + 4]
            pst = psum()
            v = pst[:, :].bitcast(BF16).rearrange("p (a b) -> p a b", a=8)
            for j, (src, dst) in enumerate(grp):
                pr, w = src.shape
                hi = abf[0:pr, j * 128:j * 128 + w]
                lo = abf[0:pr, 1024 + j * 128:1024 + j * 128 + w]
                tmp = yacc[0:pr, j * 128:j * 128 + w]
                cp(hi, src)
                tt(tmp, src, hi, ALU.subtract)
                cp(lo, tmp)
                tr(v[0:w, j, 0:pr], hi, identb[0:pr, 0:pr])
                tr(v[0:w, 4 + j, 0:pr], lo, identb[0:pr, 0:pr])
            for j, (src, dst) in enumerate(grp):
                pr, w = src.shape
                cp(dst, v[0:w, j, 0:pr])
                tt(dst, dst, v[0:w, 4 + j, 0:pr], ALU.add)

    def rms_to_xT(R, gain, dst=None):
        act(abf[0:R], hb[0:R], AF.Square, accum=col(0)[0:R])
        rstd_of(col(0)[0:R], col(1)[0:R], D)
        stt(abf[0:R], hb[0:R], col(1)[0:R], gain[0:R], ALU.mult, ALU.mult)
        transposes_to(xT if dst is None else dst, abf, 16, R)

    def post_norm(R, gain):
        act(abf[0:R], yacc[0:R], AF.Square, accum=col(2)[0:R])
        rstd_of(col(2)[0:R], col(3)[0:R], D)
        stt(yacc[0:R], yacc[0:R], col(3)[0:R], gain[0:R], ALU.mult, ALU.mult)
        tt(hb[0:R], hb[0:R], yacc[0:R], ALU.add)

    def linear_tok(W, li, ncols, R, src, evac, K=D):
        nk = K // 128
        for g in range(ncols // 512):
            ps = psum()
            for kb in range(0, nk, 8):
                wt = wtile(W[li], kb * 128, g * 512, 8, 512)
                for k in range(8):
                    mm(ps[0:R, :], src[:, kb + k, 0:R], wt[:, k, :], start=(kb + k == 0), stop=(kb + k == nk - 1))
                pump(2)
            evac(g, ps)

    def ln_rows(R, x, n, gb, bb, out, c0):
        act(junk[0:R, 0:n], x, AF.Identity, accum=col(c0)[0:R])
        act(junk[0:R, 0:n], x, AF.Square, accum=col(c0 + 1)[0:R])
        ts(col(c0)[0:R], col(c0)[0:R], 1.0 / n, None, ALU.mult)
        tt(col(c0 + 2)[0:R], col(c0)[0:R], col(c0)[0:R], ALU.mult)
        stt(col(c0 + 1)[0:R], col(c0 + 1)[0:R], 1.0 / n, col(c0 + 2)[0:R], ALU.mult, ALU.subtract)
        rstd_of(col(c0 + 1)[0:R], col(c0 + 1)[0:R], 1.0)
        ts(x, x, col(c0)[0:R], col(c0 + 1)[0:R], ALU.subtract, ALU.mult)
        dma(lnt, gb.to_broadcast([128, 1024]))
        tt(x, x, lnt[0:R], ALU.mult)
        dma(lnt, bb.to_broadcast([128, 1024]))
        tt(out, x, lnt[0:R], ALU.add)

    pending = []
    fgrp = []

    def pump(n=2):
        for _ in range(n):
            while pending:
                try:
                    next(pending[0][0])
                    break
                except StopIteration:
                    pending.pop(0)
            if not pending:
                return

    def drain():
        while pending:
            pump(1000)

    def free_slot(q):
        while any(q in sl for _, sl in pending):
            for _ in pending[0][0]:
                pass
            pending.pop(0)

    def ffn_gen(l, grp):
        for part in range(8):
            for g in range(4):
                wt = wtile(w_up[l], 0, (part * 4 + g) * 256, 16, 256)
                if len(grp) == 2 and grp[1][3] == grp[0][3] + 1:
                    q0 = grp[0][3]
                    ps = gpsum()
                    for c in range(2):
                        for k in range(16):
                            mm(ps[:, c * 256:(c + 1) * 256], wt[:, k, c * 128:(c + 1) * 128], xTfa[:, k, q0 * 128:(q0 + 2) * 128], start=(k == 0), stop=(k == 15))
                    pv = ps[:, :].rearrange("p (c r) -> p c r", c=2)
                    jv = junkF[:, :].rearrange("p (c r) -> p c r", c=2)
                    act(jv, pv, AF.Relu)
                    tt(hT[:, g * 2:(g + 1) * 2, :], jv, jv, ALU.mult)
                    yield
                    continue
                for j, (t, R, last, q) in enumerate(grp):
                    ps = gpsum()
                    for c in range(2):
                        for k in range(16):
                            mm(ps[:, c * 128:c * 128 + R], wt[:, k, c * 128:(c + 1) * 128], xTf[q][:, k, 0:R], start=(k == 0), stop=(k == 15))
                    pv = ps[:, 0:256].rearrange("p (c r) -> p c r", c=2)[:, :, 0:R]
                    jv = junkF[:, 0:256].rearrange("p (c r) -> p c r", c=2)[:, :, 0:R]
                    act(jv, pv, AF.Relu)
                    tt(hT[:, g * 2:(g + 1) * 2, j * 128:j * 128 + R], jv, jv, ALU.mult)
                yield
            for n in range(4):
                wt = wtile(w_down[l], part * 8 * 128, n * 512, 8, 512)
                for j, (t, R, last, q) in enumerate(grp):
                    ps = gpsum()
                    for k in range(8):
                        mm(ps[0:R, :], hT[:, k, j * 128:j * 128 + R], wt[:, k, :], start=(k == 0), stop=(k == 7))
                    ya = yaccF[j][0:R, n * 512:(n + 1) * 512]
                    if part == 0:
                        cp(ya, ps[0:R, :], eng="act")
                    else:
                        tt(ya, ya, ps[0:R, :], ALU.add)
                yield
        for j, (t, R, last, q) in enumerate(grp):
            jq = hT[0:R].rearrange("p a b -> p (a b)")
            ya = yaccF[j]
            act(jq, ya[0:R], AF.Square, accum=col(40)[0:R])
            rstd_of(col(40)[0:R], col(41)[0:R], D)
            gF = gain(3, l)
            stt(ya[0:R], ya[0:R], col(41)[0:R], gF[0:R], ALU.mult, ALU.mult)
            if last:
                dst = ys if t == NPT else yp[t * 128:(t + 1) * 128, :]
                key = "yout%d" % t
            else:
                dst = hscr[t, 0:R, :]
                key = "h%d" % t

            def acc_dma(e, dst=dst, src=ya[0:R]):
                return e.dma_start(out=dst, in_=src, accum_op=ALU.add)
            P.op("pool", acc_dma, reads=[ya[0:R]], writes=[key], dma=True)
            yield

    for l in range(DEPTH):
        i = l // 2
        even = (l % 2 == 0)
        curl[0] = l
        if even:
            dma(wpb, w_pool[i].rearrange("g c d -> c g d"), q="pool")
            dma(spc, s_pool[i].rearrange("(g c) -> c g", c=128))
            memset(pext, 0.0)
            memset(S, 0.0); memset(Sb, 0.0)
            dma(o_ps[i].rearrange("(s r) c -> s r c", r=15)[:, 0:11, :], spool[i].rearrange("(s r) c -> s r c", r=15)[:, 4:15, :],
                wk=["o_ps%d" % i])
        else:
            dma(junk[0:31, :], dw_w[i])
            tr32_multi([(junk[0:31, c * 128:(c + 1) * 128], dwT[:, c, :]) for c in range(8)])
            dma(dwbT, dw_b[i].rearrange("(c p) -> p c", p=128))
            dma(sgw, sg_w[i].rearrange("g a b -> a g b"))
            dma(sgbP, sg_b[i].rearrange("g a -> a g"))
            for s in range(16):
                dma(sgbS[4 * s:4 * s + 4, :], sg_b[i, :, 0:4].rearrange("g a -> a g"))
            cp(abf[:, 0:512].rearrange("p (g b) -> p g b", g=4), sgw)
            pst = psum()
            pv4 = pst[:, :].bitcast(BF16).rearrange("p (a b) -> p a b", a=8)
            for g in range(4):
                tr(pv4[:, g, :], abf[:, g * 128:(g + 1) * 128], identb)
            tt(wsT, pv4[:, 0:4, :], T["tri"].rearrange("p (o b) -> p o b", o=1).to_broadcast([128, 4, 128]), ALU.mult)
            for g in range(4):
                for s in range(16):
                    dma(junk[4 * s:4 * s + 4, 0:4], sg_w[i, g, 0:4, 0:4].rearrange("a b -> b a"))
                memset(junk[0:64, 64:128], 0.0)
                for s in range(16):
                    cp(junk[0:64, 64 + 4 * s:64 + 4 * s + 4], junk[0:64, 0:4])
                tt(wsTs[0:64, g, :], junk[0:64, 64:128], T["bd"][0:64], ALU.mult)
            memset(gext, 0.0)
            dma(o_cs[i].rearrange("(s r) c -> s r c", r=30)[:, 0:26, :], sconv[i].rearrange("(s r) c -> s r c", r=30)[:, 4:30, :],
                wk=["o_cs%d" % i])

        for t in range(NPT + 1):
            samp = (t == NPT)
            R = NS if samp else 128
            if l == 0:
                src = xs if samp else xp[t * 128:(t + 1) * 128, :]
                dma(hb[0:R], src)
            else:
                dma(hb[0:R], hscr[t, 0:R, :], rk=["h%d" % t])
            stage("load")
            rms_to_xT(R, gain(0, l))
            stage("rms0")

            if even:
                def evac_even(g, ps):
                    if g == 0:
                        cp(zbuf[0:R, 0:512], ps[0:R, :], eng=os.environ.get("KEV", "act"))
                    elif g <= 3:
                        x = ps[0:R, :].rearrange("p (h d) -> p h d", h=4)
                        cb = cosv[0:R, t:t + 1, :].to_broadcast([R, 4, 64])
                        sb_ = sinv[0:R, t:t + 1, :].to_broadcast([R, 4, 64])
                        o = qk[0:R, (g - 1) * 512:g * 512].rearrange("p (h d) -> p h d", h=4)
                        tt(rt[0][0:R, 0:4], x[:, :, 0:64], cb, ALU.mult)
                        tt(rt[1][0:R, 0:4], x[:, :, 64:128], sb_, ALU.mult)
                        tt(o[:, :, 0:64], rt[0][0:R, 0:4], rt[1][0:R, 0:4], ALU.subtract)
                        tt(rt[2][0:R, 0:4], x[:, :, 0:64], sb_, ALU.mult)
                        tt(rt[3][0:R, 0:4], x[:, :, 64:128], cb, ALU.mult)
                        tt(o[:, :, 64:128], rt[2][0:R, 0:4], rt[3][0:R, 0:4], ALU.add)
                    elif g <= 6:
                        cp(vt[0:R, (g - 4) * 512:(g - 3) * 512], ps[0:R, :], eng="act")
                    else:
                        act(zbuf[0:R, 512 + (g - 7) * 512:512 + (g - 6) * 512], ps[0:R, :], AF.Silu)
                linear_tok(w_in_even, i, 5120, R, xT, evac_even)
                stage("win")
                gs = zbuf[:, 512:2048]
                if samp:
                    for hf in range(2):
                        dma(hloadE[0:120, 0:512], spool[i, hf * 120:(hf + 1) * 120, :])
                        prs = []
                        for c in range(4):
                            prs.append((hloadE[0:120, c * 128:(c + 1) * 128], hstg[:, c, 0:120]))
                        tr32_multi(prs)
                        for c in range(4):
                            cp(pextS[:, c, :].rearrange("p (s r) -> p s r", r=19)[:, hf * 8:(hf + 1) * 8, 0:15],
                               hstg[:, c, 0:120].rearrange("p (s r) -> p s r", r=15))
                if samp:
                    for s in range(16):
                        dma(o_ps[i, s * 15 + 11:s * 15 + 15, :], zbuf[4 * s:4 * s + 4, 0:512], wk=["o_ps%d" % i])
                if samp:
                    tr32_multi([(zbuf[0:R, c * 128:(c + 1) * 128], hstg[:, c, 0:R]) for c in range(4)])
                else:
                    tr32_multi([(zbuf[0:R, c * 128:(c + 1) * 128], pext[:, c, 16:144]) for c in range(4)])
                for c in range(4):
                    w = POOLW[c]
                    if samp:
                        E = pextS[:, c, :]
                        cp(E.rearrange("p (s r) -> p s r", r=19)[:, :, 15:19], hstg[:, c, 0:64].rearrange("p (s r) -> p s r", r=4))
                        L = 304
                    else:
                        E = pext[:, c, :]
                        L = 144
                    a_, b_ = E, pA
                    sh = 1
                    while sh < w:
                        tt(b_[:, sh:L], a_[:, sh:L], a_[:, 0:L - sh], ALU.add)
                        a_ = b_
                        b_ = pB if b_ is pA else pA
                        sh *= 2
                    Ssum = a_
                    if samp:
                        stt(dT[:, c, 0:64].rearrange("p (s r) -> p s r", r=4),
                            Ssum[:, 0:304].rearrange("p (s r) -> p s r", r=19)[:, :, 15:19], 1.0 / w,
                            E.rearrange("p (s r) -> p s r", r=19)[:, :, 15:19], ALU.mult, ALU.subtract)
                    else:
                        stt(dT[:, c, :], Ssum[:, 16:144], 1.0 / w, E[:, 16:144], ALU.mult, ALU.subtract)
                        if t == 0:
                            tt(rt[0][:, 0, 0:16], Ssum[:, 16:32], T["invc"][:, c, :], ALU.mult)
                            tt(dT[:, c, 0:16], rt[0][:, 0, 0:16], E[:, 16:32], ALU.subtract)
                    ps2 = psum()
                    mm(ps2[:, 0:R], wpb[:, c, :], dT[:, c, 0:R])
                    act(yT[:, c, 0:R], ps2[:, 0:R], AF.Copy, scale=spc[:, c:c + 1])
                    if not samp:
                        cp(E[:, 1:16], E[:, 129:144])
                    pump(2)
                if t == NPT - 1:
                    dma(o_pp[i], zbuf[113:128, 0:512], wk=["o_pp%d" % i])
                stage("pool")
                psq = psum(); psk = psum()
                qv = psq[:, :].bitcast(BF16).rearrange("p (a b) -> p a b", a=8)
                kv = psk[:, :].bitcast(BF16).rearrange("p (a b) -> p a b", a=8)
                for h in range(HEADS):
                    tr(qv[:, h, 0:R], qk[0:R, h * 128:(h + 1) * 128], identb[0:R, 0:R])
                    tr(kv[:, h, 0:R], qk[0:R, 768 + h * 128:768 + (h + 1) * 128], identb[0:R, 0:R])
                cp(qT[:, :, 0:R], qv[:, 0:HEADS, 0:R], eng="act")
                tt(qdT[:, :, 0:R], qv[:, 0:HEADS, 0:R], (T["qdecS"] if samp else T["qdecT"])[:, :, 0:R], ALU.mult)
                cp(kT[:, :, 0:R], kv[:, 0:HEADS, 0:R], eng="act")
                kdt = T["kdecS"] if samp else T["kdec"]
                tt(kd[0:R], qk[0:R, 768:1536].rearrange("p (h d) -> p h d", h=HEADS),
                   kdt[0:R].rearrange("p (h o) -> p h o", o=1).to_broadcast([R, HEADS, 128]), ALU.mult)
                psc = [psum(), psum()]
                for h in range(HEADS):
                    mm(psc[h // 4][0:R, (h % 4) * 128:(h % 4) * 128 + R], kT[:, h, 0:R], qT[:, h, 0:R])
                mk = maskSb if samp else maskTb
                tt(PT[0:R, 0:4, 0:R], psc[0][0:R, :].rearrange("p (h l) -> p h l", h=4)[:, :, 0:R], mk[0:R, 0:4, 0:R], ALU.mult)
                tt(PT[0:R, 4:6, 0:R], psc[1][0:R, 0:256].rearrange("p (h l) -> p h l", h=2)[:, :, 0:R], mk[0:R, 4:6, 0:R], ALU.mult)
                if not samp:
                    pso = [psum(), psum(), psum()]

                    def oap(h):
                        return pso[h // 2][0:R, (h % 2) * 256:(h % 2 + 1) * 256]
                    for h in range(HEADS):
                        mm(oap(h), PT[0:R, h, 0:R], vt[0:R, h * 256:(h + 1) * 256], start=True, stop=False)
                        mm(oap(h), qdT[:, h, 0:R], Sb[:, h, :], start=False, stop=True)
                    pss = [psum(), psum(), psum()]
                    for h in range(HEADS):
                        sp_ = pss[h // 2][:, (h % 2) * 256:(h % 2 + 1) * 256]
                        mm(sp_, kd[0:R, h, :], vt[0:R, h * 256:(h + 1) * 256])
                        stt(S[:, h, :], S[:, h, :], float(cd[h]), sp_, ALU.mult, ALU.add)
                        cp(Sb[:, h, :], S[:, h, :], eng="act")
                    if t == NPT - 1:
                        dma(o_rp[i].rearrange("h d e -> d h e"), S, wk=["o_rp%d" % i])
                else:
                    oacc = yacc[0:64, 0:1536]
                    pso = [psum(), psum(), psum()]
                    for h in range(HEADS):
                        oh = pso[h // 2][0:R, (h % 2) * 256:(h % 2 + 1) * 256]
                        mm(oh, PT[0:R, h, 0:R], vt[0:R, h * 256:(h + 1) * 256])
                    for hp in range(3):
                        cp(oacc[:, hp * 512:(hp + 1) * 512], pso[hp][0:R, :], eng="act")
                    for s in range(16):
                        dma(S0, sret[i, s].rearrange("h d e -> d h e"))
                        cp(S0b, S0, eng="act")
                        tt(qdm, qdT[:, :, 0:64], T["colmask"][:, s:s + 1, :].to_broadcast([128, HEADS, 64]), ALU.mult)
                        ts(kdm[0:64], kd[0:64].rearrange("p h d -> p (h d)"), T["onehot"][0:64, s:s + 1], None, ALU.mult)
                        pso = [psum(), psum(), psum()]
                        for h in range(HEADS):
                            oh = pso[h // 2][0:R, (h % 2) * 256:(h % 2 + 1) * 256]
                            mm(oh, qdm[:, h, :], S0b[:, h, :])
                        for hp in range(3):
                            tt(oacc[:, hp * 512:(hp + 1) * 512], oacc[:, hp * 512:(hp + 1) * 512], pso[hp][0:R, :], ALU.add)
                        pss = [psum(), psum(), psum()]
                        for h in range(HEADS):
                            sp_ = pss[h // 2][:, (h % 2) * 256:(h % 2 + 1) * 256]
                            mm(sp_, kdm[0:64, h * 128:(h + 1) * 128], vt[0:64, h * 256:(h + 1) * 256])
                            stt(S0[:, h, :], S0[:, h, :], float(cd4[h]), sp_, ALU.mult, ALU.add)
                        dma(o_rs[i, s].rearrange("h d e -> d h e"), S0, wk=["o_rs%d_%d" % (i, s)])
                        pump(3)

                    def oap(h):
                        return oacc[:, h * 256:(h + 1) * 256]
                stage("ret")
                for h in range(HEADS):
                    act(junk[0:R, 0:256], oap(h), AF.Square, accum=col(8 + h)[0:R])
                act(small[0:R, 16:22], small[0:R, 8:14], AF.Sqrt, bias=EPS, scale=1.0 / 256)
                recip(small[0:R, 16:22], small[0:R, 16:22])
                for h in range(HEADS):
                    stt(yin[0:R, h * 256:(h + 1) * 256], oap(h), col(16 + h)[0:R], gs[0:R, h * 256:(h + 1) * 256], ALU.mult, ALU.mult)
                transposes_to(yT, yin, 12, R, dst_chunk0=4)
                Wout = w_out_even
            else:
                def evac_odd(g, ps):
                    if g < 4:
                        act(zbuf[0:R, g * 512:(g + 1) * 512], ps[0:R, :], AF.Gelu_apprx_tanh)
                    elif g < 6:
                        cp(zbuf[0:R, g * 512:(g + 1) * 512], ps[0:R, :], eng="act")
                    else:
                        act(junk[0:R, 0:512], ps[0:R, :], AF.Sigmoid)
                        tt(zbuf[0:R, (g - 2) * 512:(g - 1) * 512], zbuf[0:R, (g - 2) * 512:(g - 1) * 512], junk[0:R, 0:512], ALU.mult)
                linear_tok(w_in_odd, i, 4096, R, xT, evac_odd)
                ug = zbuf[:, 0:1024]; vv = zbuf[:, 1024:2048]; glu = zbuf[:, 2048:3072]
                ln_rows(R, vv[0:R], 1024, sg_ln_g[i:i + 1, :], sg_ln_b[i:i + 1, :], vv[0:R], 24)
                if samp:
                    dma(o_sv[i], vv[0:64], wk=["o_sv%d" % i])
                    for s in range(16):
                        dma(o_cs[i, s * 30 + 26:s * 30 + 30, :], glu[4 * s:4 * s + 4], wk=["o_cs%d" % i])
                cp(abf[0:R, 0:1024], vv[0:R], eng="act")
                for g in range(4):
                    ps = psum()
                    if samp:
                        mm(ps[0:64, 0:256], wsTs[0:64, g, :], abf[0:64, g * 256:(g + 1) * 256])
                        bcol = sgbS[0:64, g:g + 1]
                    else:
                        mm(ps[:, 0:256], wsT[:, g, :], abf[:, g * 256:(g + 1) * 256])
                        bcol = sgbP[:, g:g + 1]
                    stt(yin[0:R, g * 256:(g + 1) * 256], ps[0:R, 0:256], bcol, ug[0:R, g * 256:(g + 1) * 256], ALU.add, ALU.mult)
                if samp:
                    for q4 in range(4):
                        dma(hloadO[0:120, :], sconv[i, q4 * 120:(q4 + 1) * 120, :])
                        tr32_multi([(hloadO[0:120, c * 128:(c + 1) * 128], hstg[:, c, 0:120]) for c in range(8)])
                        for c in range(8):
                            cp(gextS[:, c, :].rearrange("p (s r) -> p s r", r=34)[:, q4 * 4:(q4 + 1) * 4, 0:30],
                               hstg[:, c, 0:120].rearrange("p (s r) -> p s r", r=30))
                if samp:
                    tr32_multi([(glu[0:R, c * 128:(c + 1) * 128], hstg[:, c, 0:R]) for c in range(8)])
                else:
                    tr32_multi([(glu[0:R, c * 128:(c + 1) * 128], gext[:, c, 32:160]) for c in range(8)])
                for c in range(8):
                    if samp:
                        E3 = gextS[:, c, :].rearrange("p (s r) -> p s r", r=34)
                        cp(E3[:, :, 30:34], hstg[:, c, 0:64].rearrange("p (s r) -> p s r", r=4))
                        acc = cacc[:, c, 0:64].rearrange("p (s r) -> p s r", r=4)
                        ts(acc, E3[:, :, 0:4], dwT[:, c, 0:1], dwbT[:, c:c + 1], ALU.mult, ALU.add)
                        for j in range(1, 31):
                            stt(acc, E3[:, :, j:j + 4], dwT[:, c, j:j + 1], acc, ALU.mult, ALU.add)
                    else:
                        E = gext[:, c, :]
                        acc = cacc[:, c, :]
                        ts(acc, E[:, 2:130], dwT[:, c, 0:1], dwbT[:, c:c + 1], ALU.mult, ALU.add)
                        for j in range(1, 31):
                            stt(acc, E[:, 2 + j:130 + j], dwT[:, c, j:j + 1], acc, ALU.mult, ALU.add)
                        cp(E[:, 2:32], E[:, 130:160])
                    pump(3)
                if t == NPT - 1:
                    dma(o_cp[i], glu[98:128], wk=["o_cp%d" % i])
                tr32_multi([(cacc[:, c, 0:R], zbuf[0:R, 3072 + c * 128:3072 + (c + 1) * 128]) for c in range(8)])
                cv = zbuf[:, 3072:4096]
                ln_rows(R, cv[0:R], 1024, cv_ln_g[i:i + 1, :], cv_ln_b[i:i + 1, :], cv[0:R], 28)
                act(yin[0:R, 1024:2048], cv[0:R], AF.Silu)
                transposes_to(yT, yin, 16, R)
                Wout = w_out_odd

            stage("mix")
            def evac_y(g, ps):
                cp(yacc[0:R, g * 512:(g + 1) * 512], ps[0:R, :], eng="act")
            linear_tok(Wout, i, D, R, yT, evac_y)
            q = t % 3
            free_slot(q)
            post_norm(R, gain(1, l))
            stage("wout")
            last = (l == DEPTH - 1)
            if last:
                dma(ys if samp else yp[t * 128:(t + 1) * 128, :], hb[0:R], wk=["yout%d" % t])
            else:
                dma(hscr[t, 0:R, :], hb[0:R], wk=["h%d" % t])
            rms_to_xT(R, gain(2, l), dst=xTf[q])
            fgrp.append((t, R, last, q))
            if len(fgrp) == 2 or samp:
                pending.append([ffn_gen(l, list(fgrp)), set(e_[3] for e_ in fgrp)])
                del fgrp[:]
            stage("ffn")
    drain()

    for q in ("sp", "pool"):
        n = P.ndma[q]
        toks = [("d", q, k) for k in range(max(0, n - RING[q]), n)]
        waits = [tk for tk in toks if tk not in P.dknown["sp"]]
        P.ops["sp"].append((waits, None, ("c", "sp", len(P.ops["sp"]))))

    with nc.Block() as block:
        P.emit(nc, block, sems, dsems)
    es.close()
    return nc


_CACHE = {}


def kernel(**inp):
    tabs, cd, cd4 = host_tables()
    if "nc" not in _CACHE:
        _CACHE["nc"] = build(cd, cd4)
    nc = _CACHE["nc"]
    f = lambda a: np.ascontiguousarray(np.asarray(a, dtype=np.float32))
    norms = f(np.concatenate([inp["norm_mix_pre"], inp["norm_mix_post"], inp["norm_ffn_pre"], inp["norm_ffn_post"]], 0))
    shared = {k: f(inp[k]) for k in ("w_in_even", "w_pool", "s_pool", "w_out_even", "w_in_odd", "sg_ln_g", "sg_ln_b", "sg_w",
                                     "sg_b", "dw_w", "dw_b", "cv_ln_g", "cv_ln_b", "w_out_odd", "w_up", "w_down")}
    shared["norms"] = norms
    for k, v in tabs.items():
        shared["t_" + k] = f(v)
    xpr = f(inp["x_prompt"]); xsm = f(inp["x_sample"])
    sp_ = f(inp["state_pool"]); sr_ = f(inp["state_ret"]); sc_ = f(inp["state_conv"])
    in_maps = []
    for c in range(8):
        m = dict(shared)
        m["xp"] = xpr[c % 4]
        m["xs"] = xsm[16 * c:16 * c + 16].reshape(NS, D)
        m["spool"] = np.ascontiguousarray(sp_[:, 16 * c:16 * c + 16].reshape(2, 240, 512))
        m["sret"] = np.ascontiguousarray(sr_[:, 16 * c:16 * c + 16])
        m["sconv"] = np.ascontiguousarray(sc_[:, 16 * c:16 * c + 16].reshape(2, 480, 1024))
        in_maps.append(m)
    res = run_bass_kernel_spmd(nc, in_maps, core_ids=list(range(8)))
    R_ = res.results
    y_prompt = np.stack([R_[b]["yp"] for b in range(4)], 0)
    y_sample = np.concatenate([R_[c]["ys"].reshape(16, 4, D) for c in range(8)], 0)
    pool_p = np.stack([R_[b]["o_pp"] for b in range(4)], 1)
    pool_s = np.concatenate([R_[c]["o_ps"].reshape(2, 16, 15, 512) for c in range(8)], 1)
    ret_p = np.stack([R_[b]["o_rp"] for b in range(4)], 1)
    ret_s = np.concatenate([R_[c]["o_rs"] for c in range(8)], 1)
    conv_p = np.stack([R_[b]["o_cp"] for b in range(4)], 1)
    conv_s = np.concatenate([R_[c]["o_cs"].reshape(2, 16, 30, 1024) for c in range(8)], 1)
    sgv_s = np.concatenate([R_[c]["o_sv"].reshape(2, 16, 4, 1024) for c in range(8)], 1)
    outs = (y_prompt, y_sample, pool_p, pool_s, ret_p, ret_s, conv_p, conv_s, sgv_s)
    return tuple(np.ascontiguousarray(o, dtype=np.float32) for o in outs)
```

```python
import numpy as np
import concourse.bass as bass
import concourse.mybir as mybir
from concourse.bass_utils import run_bass_kernel_spmd

F32 = mybir.dt.float32
BF16 = mybir.dt.bfloat16
AF = mybir.ActivationFunctionType
ALU = mybir.AluOpType

D = 2048
DEPTH = 4
SEQ = 2048
NPT = SEQ // 128
NS = 64
HEADS = 6
EPS = 1e-6
POOLW = (2, 4, 8, 16)
ENGS = ("pe", "act", "dve", "pool", "sp")
RING = {"sp": 12, "pool": 6}
BLK = 256


class Prog:
    def __init__(self):
        self.ops = {e: [] for e in ENGS}
        self.clock = {e: {f: -1 for f in ENGS} for e in ENGS}
        self.dknown = {e: set() for e in ENGS}
        self.snap = {}
        self.state = {}
        self.ndma = {"sp": 0, "pool": 0}
        self.signal = set()
        self.cache = {}

    def keys(self, x):
        if isinstance(x, str):
            return (x,)
        name = x.tensor.name
        if name != "SB":
            return (name,)
        esz = 4 if x.dtype == F32 else 2
        ck = (x.offset, x.ap, esz)
        r = self.cache.get(ck)
        if r is not None:
            return r
        ap = x.ap
        pstep = ap[0][0]
        off = x.offset % pstep
        dims = [(s, c) for (s, c) in ap[1:] if c > 1]
        if not dims:
            ivs = [(off, off + 1)]
        else:
            inner = dims[-1]
            outer = dims[:-1]
            ncomb = 1
            for s, c in outer:
                ncomb *= c
            if inner[0] == 1 and ncomb <= 512:
                starts = [off]
                for s, c in outer:
                    starts = [b + s * i for b in starts for i in range(c)]
                ivs = [(b, b + inner[1]) for b in starts]
            else:
                hi = off + sum(s * (c - 1) for s, c in dims) + 1
                ivs = [(off, hi)]
        ks = set()
        for lo, hi in ivs:
            for b in range((lo * esz) // BLK, ((hi * esz) - 1) // BLK + 1):
                ks.add(b)
        r = tuple(ks)
        self.cache[ck] = r
        return r

    dead = False
    maxops = 0

    def op(self, eng, fn, reads=(), writes=(), dma=False):
        if self.dead:
            return None
        self.nops = getattr(self, "nops", 0) + 1
        if self.maxops and self.nops > self.maxops:
            self.dead = True
            return None
        deps = set()
        rk = [k for x in reads for k in self.keys(x)]
        wk = [k for x in writes for k in self.keys(x)]
        psr = [k for k in rk if isinstance(k, str) and k.startswith("PS")]
        if psr:
            rk = [k for k in rk if not (isinstance(k, str) and k.startswith("PS"))]
            wk = wk + psr
        for k in rk:
            st = self.state.get(k)
            if st and st[0] is not None:
                deps.add(st[0])
        for k in wk:
            st = self.state.get(k)
            if st:
                if st[0] is not None:
                    deps.add(st[0])
                deps.update(st[1].values())
        idx = len(self.ops[eng])
        if dma:
            n = self.ndma[eng]
            self.ndma[eng] = n + 1
            tok = ("d", eng, n)
            if n >= RING[eng]:
                deps.add(("d", eng, n - RING[eng]))
        else:
            tok = ("c", eng, idx)
        clk = self.clock[eng]
        best = {}
        ddeps = []
        for dtok in deps:
            if dtok[0] == "c":
                _, f, i = dtok
                if f == eng and eng in ("pe", "sp"):
                    continue
                if clk[f] >= i:
                    continue
                if f not in best or best[f] < i:
                    best[f] = i
            else:
                if dtok not in self.dknown[eng]:
                    ddeps.append(dtok)
        waits = [("c", f, i) for f, i in best.items()] + ddeps
        for dtok in waits:
            if dtok[0] == "c":
                self.signal.add(dtok)
                sn = self.snap[dtok]
                for f in ENGS:
                    if sn[f] > clk[f]:
                        clk[f] = sn[f]
                if dtok[2] > clk[dtok[1]]:
                    clk[dtok[1]] = dtok[2]
            else:
                self.dknown[eng].add(dtok)
                sn = self.snap[dtok]
                for f in ENGS:
                    if sn[f] > clk[f]:
                        clk[f] = sn[f]
        sn = dict(clk)
        self.snap[tok] = sn
        self.ops[eng].append((waits, fn, tok))
        for k in rk:
            st = self.state.setdefault(k, [None, {}])
            if tok[0] == "c":
                st[1][(tok[0], tok[1])] = tok
            else:
                st[1][tok] = tok
        for k in wk:
            self.state[k] = [tok, {}]
        return tok

    def emit(self, nc, block, sems, dsems):
        semval = {}
        for e in ENGS:
            cnt = 0
            for i, (_, _, tok) in enumerate(self.ops[e]):
                if tok in self.signal:
                    cnt += 1
                    semval[tok] = cnt

        def run(e, engobj):
            for waits, fn, tok in self.ops[e]:
                for w in waits:
                    if w[0] == "c":
                        engobj.wait_ge(sems[w[1]], semval[w])
                    else:
                        r = RING[w[1]]
                        engobj.wait_ge(dsems[w[1]][w[2] % r], 16 * (w[2] // r + 1))
                if fn is None:
                    continue
                ins = fn(engobj)
                if tok[0] == "d":
                    ins.then_inc(dsems[tok[1]][tok[2] % RING[tok[1]]], 16)
                elif tok in self.signal:
                    ins.then_inc(sems[e], 1)

        block.tensor(lambda g: run("pe", g))
        block.scalar(lambda g: run("act", g))
        block.vector(lambda g: run("dve", g))
        block.gpsimd(lambda g: run("pool", g))
        block.sync(lambda g: run("sp", g))


def host_tables(NPT=NPT):
    SEQ = NPT * 128
    half = 64
    inv = (10000.0 ** (-np.arange(half, dtype=np.float32) / np.float32(half))).astype(np.float32)
    pos_p = np.arange(SEQ, dtype=np.float32)
    pos_s = (16384 + np.arange(4)).astype(np.float32)
    pos_s = np.tile(pos_s, 16)
    ang_p = (pos_p[:, None] * inv[None, :]).astype(np.float32)
    ang_s = (pos_s[:, None] * inv[None, :]).astype(np.float32)
    cosT = np.zeros((128, NPT + 1, 64), np.float32)
    sinT = np.zeros((128, NPT + 1, 64), np.float32)
    cosT[:, :NPT] = np.cos(ang_p).astype(np.float32).reshape(NPT, 128, 64).transpose(1, 0, 2)
    sinT[:, :NPT] = np.sin(ang_p).astype(np.float32).reshape(NPT, 128, 64).transpose(1, 0, 2)
    cosT[:64, NPT] = np.cos(ang_s).astype(np.float32)
    sinT[:64, NPT] = np.sin(ang_s).astype(np.float32)
    hh = np.arange(HEADS, dtype=np.float32)
    log_g = np.log1p(-np.exp2(-5.0 - hh)).astype(np.float32)
    scale = np.float32(128 ** -0.5)
    idx = np.arange(128, dtype=np.float32)
    diff = idx[:, None] - idx[None, :]
    mask = np.where(diff[None] >= 0, np.exp(log_g[:, None, None] * np.maximum(diff, 0.0)[None]), 0.0)
    maskT = (mask.transpose(2, 0, 1) * scale).astype(np.float32)
    i4 = np.arange(4, dtype=np.float32)
    d4 = i4[:, None] - i4[None, :]
    m4 = np.where(d4[None] >= 0, np.exp(log_g[:, None, None] * np.maximum(d4, 0.0)[None]), 0.0)
    maskS = np.zeros((64, HEADS, 64), np.float32)
    for s in range(16):
        maskS[4 * s:4 * s + 4, :, 4 * s:4 * s + 4] = m4.transpose(2, 0, 1) * scale
    qdec = np.exp(log_g[:, None] * (idx + 1.0)[None, :]).astype(np.float32)
    qdecT = np.broadcast_to(qdec[None], (128, HEADS, 128)).copy()
    qdec4 = np.exp(log_g[:, None] * (i4 + 1.0)[None, :]).astype(np.float32)
    qdecS = np.broadcast_to(np.tile(qdec4, (1, 16))[None], (128, HEADS, 64)).copy()
    kdec = (np.exp(log_g[None, :] * (127.0 - idx)[:, None]) * scale).astype(np.float32)
    kdec4 = (np.exp(log_g[None, :] * (3.0 - i4)[:, None]) * scale).astype(np.float32)
    kdecS = np.zeros((128, HEADS), np.float32)
    kdecS[:64] = np.tile(kdec4, (16, 1))
    colmask = np.zeros((128, 16, 64), np.float32)
    onehot = np.zeros((128, 16), np.float32)
    for s in range(16):
        colmask[:, s, 4 * s:4 * s + 4] = 1.0
        onehot[4 * s:4 * s + 4, s] = 1.0
    invc = np.zeros((128, 4, 16), np.float32)
    for g, w in enumerate(POOLW):
        invc[:, g, :] = 1.0 / np.minimum(w, np.arange(16) + 1.0)
    tri = np.tril(np.ones((128, 128), np.float32)).T.copy()
    bd = np.zeros((128, 64), np.float32)
    for s in range(16):
        for j in range(4):
            for i in range(j, 4):
                bd[4 * s + j, 4 * s + i] = 1.0
    cd = np.exp(log_g * 128.0).astype(np.float32)
    cd4 = np.exp(log_g * 4.0).astype(np.float32)
    return dict(cosT=cosT, sinT=sinT, maskT=maskT, maskS=np.concatenate([maskS, np.zeros_like(maskS)], 0),
                qdecT=qdecT, qdecS=qdecS, kdec=kdec, kdecS=kdecS, colmask=colmask, onehot=onehot,
                invc=invc, tri=tri, bd=bd), cd, cd4


def table_shapes(NPT):
    return dict(cosT=[128, NPT + 1, 64], sinT=[128, NPT + 1, 64], maskT=[128, HEADS, 128], maskS=[128, HEADS, 64],
                qdecT=[128, HEADS, 128], qdecS=[128, HEADS, 64], kdec=[128, HEADS], kdecS=[128, HEADS],
                    colmask=[128, 16, 64], onehot=[128, 16], invc=[128, 4, 16], tri=[128, 128], bd=[128, 64])


def build(cd, cd4, DEPTH=DEPTH, NPT=NPT):
    SEQ = NPT * 128
    NE = (DEPTH + 1) // 2
    NO = max(DEPTH // 2, 1)
    nc = bass.Bass("TRN2", target_bir_lowering=False)
    P = Prog()

    def din(name, shape):
        return nc.dram_tensor(name, list(shape), F32, kind="ExternalInput").ap()

    def dout(name, shape):
        return nc.dram_tensor(name, list(shape), F32, kind="ExternalOutput").ap()

    xp = din("xp", [SEQ, D]); xs = din("xs", [NS, D])
    spool = din("spool", [NE, 240, 512]); sret = din("sret", [NE, 16, HEADS, 128, 256]); sconv = din("sconv", [NO, 480, 1024])
    norms = din("norms", [4 * DEPTH, D])
    w_in_even = din("w_in_even", [NE, D, 5120]); w_pool = din("w_pool", [NE, 4, 128, 128]); s_pool = din("s_pool", [NE, 512])
    w_out_even = din("w_out_even", [NE, D, D]); w_in_odd = din("w_in_odd", [NO, D, 4096])
    sg_ln_g = din("sg_ln_g", [NO, 1024]); sg_ln_b = din("sg_ln_b", [NO, 1024]); sg_w = din("sg_w", [NO, 4, 128, 128]); sg_b = din("sg_b", [NO, 4, 128])
    dw_w = din("dw_w", [NO, 31, 1024]); dw_b = din("dw_b", [NO, 1024]); cv_ln_g = din("cv_ln_g", [NO, 1024]); cv_ln_b = din("cv_ln_b", [NO, 1024])
    w_out_odd = din("w_out_odd", [NO, D, D]); w_up = din("w_up", [DEPTH, D, 8192]); w_down = din("w_down", [DEPTH, 8192, D])
    TABLE_SHAPES = table_shapes(NPT)
    tabs = {k: din("t_" + k, v) for k, v in TABLE_SHAPES.items()}
    yp = dout("yp", [SEQ, D]); ys = dout("ys", [NS, D])
    o_pp = dout("o_pp", [NE, 15, 512]); o_ps = dout("o_ps", [NE, 240, 512])
    o_rp = dout("o_rp", [NE, HEADS, 128, 256]); o_rs = dout("o_rs", [NE, 16, HEADS, 128, 256])
    o_cp = dout("o_cp", [NO, 30, 1024]); o_cs = dout("o_cs", [NO, 480, 1024]); o_sv = dout("o_sv", [NO, NS, 1024])
    hscr = nc.dram_tensor("hscr", [NPT + 1, 128, D], F32).ap()

    SBW = 103 * 1024
    import contextlib
    es = contextlib.ExitStack()
    SB = es.enter_context(nc.sbuf_tensor("SB", [128, SBW], BF16))
    banks = [es.enter_context(nc.psum_tensor("PS%d" % i, [128, 512], F32)) for i in range(8)]
    sems = {e: es.enter_context(nc.semaphore("s_" + e)) for e in ENGS}
    dsems = {q: [es.enter_context(nc.semaphore("d_%s%d" % (q, i))) for i in range(RING[q])] for q in RING}
    nc.allow_low_precision("bf16 matmul operands, fp32 accumulation")
    es.enter_context(nc.allow_non_contiguous_dma("small strided parameter loads"))

    import os
    kstop = os.environ.get("KSTOP", "")

    P.maxops = int(os.environ.get("KMAXOPS", "0"))

    def stage(name):
        if os.environ.get("KVERBOSE"):
            print("stage", name, "nops", getattr(P, "nops", 0))
        if kstop and name == kstop:
            P.dead = True

    cur = [0]

    def alloc(dt, *dims):
        n = 1
        for d_ in dims:
            n *= d_
        nb = n * (4 if dt == F32 else 2)
        nb = (nb + BLK - 1) // BLK * BLK
        off = cur[0]
        cur[0] += nb
        assert cur[0] <= SBW * 2, "SBUF overflow"
        v = SB[:, off // 2:(off + nb) // 2]
        if dt == F32:
            v = v.bitcast(F32)
        v = v[:, 0:n]
        if len(dims) == 2:
            v = v.rearrange("p (a b) -> p a b", a=dims[0])
        elif len(dims) == 3:
            v = v.rearrange("p (a b c) -> p a b c", a=dims[0], b=dims[1])
        return v

    pb = [0]

    held = set()

    def psum():
        k = pb[0] % 6
        pb[0] += 1
        return banks[k]

    gpb = [0]

    def gpsum():
        k = 6 + gpb[0] % 2
        gpb[0] += 1
        return banks[k]

    def hold(n):
        out = []
        for _ in range(n):
            while pb[0] % 8 in held:
                pb[0] += 1
            k = pb[0] % 8
            pb[0] += 1
            held.add(k)
            out.append(banks[k])
        return out

    def release(bs):
        for b in bs:
            held.discard(banks.index(b))

    def mm(out, lhsT, rhs, start=True, stop=True):
        P.op("pe", lambda e: e.matmul(out, lhsT, rhs, start=start, stop=stop), reads=[lhsT, rhs], writes=[out])

    def tr(out, in_, ident):
        P.op("pe", lambda e: e.transpose(out, in_, ident), reads=[in_, ident], writes=[out])

    def act(out, in_, func, bias=None, scale=None, accum=None):
        kw = {}
        rd = [in_]
        if bias is not None:
            kw["bias"] = bias
            if not isinstance(bias, float):
                rd.append(bias)
        if scale is not None:
            kw["scale"] = scale
            if not isinstance(scale, float):
                rd.append(scale)
        wr = [out]
        if accum is not None:
            kw["accum_out"] = accum
            wr.append(accum)
            P.op("dve", lambda e: e.memset(accum, 0.0), writes=[accum])
        P.op("act", lambda e: e.activation(out, in_, func, **kw), reads=rd, writes=wr)

    def tt(out, in0, in1, op, eng="dve"):
        P.op(eng, lambda e: e.tensor_tensor(out, in0, in1, op), reads=[in0, in1], writes=[out])

    def ts(out, in0, s1, s2, op0, op1=None, eng="dve"):
        rd = [in0] + [s for s in (s1, s2) if s is not None and not isinstance(s, float)]
        if op1 is None:
            P.op(eng, lambda e: e.tensor_scalar(out, in0, s1, None, op0), reads=rd, writes=[out])
        else:
            P.op(eng, lambda e: e.tensor_scalar(out, in0, s1, s2, op0, op1), reads=rd, writes=[out])

    def stt(out, in0, scalar, in1, op0, op1, eng="dve"):
        rd = [in0, in1] + ([] if isinstance(scalar, float) else [scalar])
        P.op(eng, lambda e: e.scalar_tensor_tensor(out, in0, scalar, in1, op0, op1), reads=rd, writes=[out])

    def cp(out, in_, eng="dve"):
        if eng == "act":
            P.op("act", lambda e: e.activation(out, in_, AF.Identity), reads=[in_], writes=[out])
        else:
            P.op(eng, lambda e: e.tensor_copy(out, in_), reads=[in_], writes=[out])

    def recip(out, in_):
        P.op("dve", lambda e: e.reciprocal(out, in_), reads=[in_], writes=[out])

    def memset(ap, val, eng="dve"):
        P.op(eng, lambda e: e.memset(ap, val), writes=[ap])

    def dma(out, in_, q="sp", rk=(), wk=()):
        rd = list(rk) + ([in_] if in_.tensor.name == "SB" else [])
        wr = list(wk) + ([out] if out.tensor.name == "SB" else [])
        return P.op(q, lambda e: e.dma_start(out=out, in_=in_), reads=rd, writes=wr, dma=True)

    for z0 in range(0, SBW, 16384):
        z1 = min(SBW, z0 + 16384)
        memset(SB[:, z0:z1], 0.0, eng="pool" if (z0 // 16384) % 2 else "dve")
    identf = alloc(F32, 128)
    identb = alloc(BF16, 128)
    P.op("pool", lambda e: e.memset(identf, 0.0), writes=[identf])
    P.op("pool", lambda e: e.affine_select(out=identf, in_=identf, compare_op=ALU.not_equal, fill=1.0, base=0,
                                           pattern=[[-1, 128]], channel_multiplier=1), reads=[identf], writes=[identf])
    cp(identb, identf)
    T = {}
    for k, shp in TABLE_SHAPES.items():
        T[k] = alloc(F32, *shp[1:])
        dma(T[k], tabs[k])
    maskTb = alloc(BF16, HEADS, 128); cp(maskTb, T["maskT"])
    maskSb = alloc(BF16, HEADS, 64); cp(maskSb, T["maskS"])
    cosv = T["cosT"]; sinv = T["sinT"]

    stage("init")
    hb = alloc(F32, D); yacc = alloc(F32, D); abf = alloc(BF16, D); junk = alloc(F32, 1024)
    xT = alloc(BF16, 16, 128); yT = alloc(BF16, 16, 128); yin = alloc(BF16, D)
    gbc = [alloc(F32, D)]
    xTfa = alloc(BF16, 16, 256); xTf = [xTfa[:, :, 0:128], xTfa[:, :, 128:256]]
    yaccF = [alloc(F32, D) for _ in range(2)]; junkF = alloc(F32, 512); hT = alloc(BF16, 8, 256)
    gn = [0]
    WR = [alloc(BF16, 4096) for _ in range(4)]
    wn = [0]
    small = alloc(F32, 64)
    zbuf = alloc(F32, 4096)
    rt = [alloc(F32, 6, 64) for _ in range(4)]
    hstg = alloc(F32, 8, 128)
    base1 = cur[0]
    S = alloc(F32, HEADS, 256); Sb = alloc(BF16, HEADS, 256)
    pext = alloc(F32, 4, 144); wpb = alloc(BF16, 4, 128); spc = alloc(F32, 4)
    e1 = cur[0]
    cur[0] = base1
    gext = alloc(F32, 8, 160); dwT = alloc(F32, 8, 31); dwbT = alloc(F32, 8)
    lnt = alloc(F32, 1024)
    wsT = alloc(BF16, 4, 128); wsTs = alloc(BF16, 4, 64); sgw = alloc(F32, 4, 128); sgbP = alloc(F32, 4); sgbS = alloc(F32, 4)
    base2 = max(e1, cur[0])
    cur[0] = base2
    qk = alloc(BF16, 1536); vt = alloc(BF16, 1536)
    qT = alloc(BF16, HEADS, 128); qdT = alloc(BF16, HEADS, 128); kT = alloc(BF16, HEADS, 128); kd = alloc(BF16, HEADS, 128)
    PT = alloc(BF16, HEADS, 128)
    S0 = zbuf[:, 2048:3584].rearrange("p (h e) -> p h e", h=HEADS); S0b = alloc(BF16, HEADS, 256); qdm = alloc(BF16, HEADS, 64); kdm = alloc(BF16, HEADS * 128)
    pextS = alloc(F32, 4, 304); pA = alloc(F32, 304); pB = alloc(F32, 304)
    dT = alloc(BF16, 4, 128); hloadE = alloc(F32, 512)
    e2 = cur[0]
    cur[0] = base2
    gextS = alloc(F32, 8, 16 * 34); cacc = alloc(F32, 8, 128); hloadO = alloc(F32, 1024)
    e3 = cur[0]
    cur[0] = base2
    pass
    cur[0] = max(e2, e3, cur[0])

    print("SBUF bytes used", cur[0], "of", SBW * 2)

    def gain(j, l_):
        b = gbc[0]
        gn[0] += 1
        r = DEPTH * j + l_
        dma(b, norms[r:r + 1, :].to_broadcast([128, D]))
        return b

    def col(i):
        return small[:, i:i + 1]

    wscr = [nc.dram_tensor("wscr%d" % l_, [96, 128, 4096], BF16).ap() for l_ in range(DEPTH)]
    wids = {}
    wcnt = [0] * DEPTH
    curl = [0]

    def wtile(W, r0, c0, nk, ncols):
        b = WR[wn[0] % len(WR)]
        wn[0] += 1
        bv = b.rearrange("p (k n) -> p k n", k=nk)
        key = (W.tensor.name, W.offset, r0, c0, nk)
        if key not in wids:
            wid = (curl[0], wcnt[curl[0]])
            wcnt[curl[0]] += 1
            wids[key] = wid
            dma(bv, W[r0:r0 + nk * 128, c0:c0 + ncols].rearrange("(k p) n -> p k n", p=128), q="pool")
            dma(wscr[wid[0]][wid[1]], b, q="sp", wk=["wscr%d_%d" % wid])
        else:
            wid = wids[key]
            dma(b, wscr[wid[0]][wid[1]], q="pool", rk=["wscr%d_%d" % wid])
        return bv

    def rstd_of(ss_col, out_col, n):
        act(out_col, ss_col, AF.Sqrt, bias=EPS, scale=1.0 / n)
        recip(out_col, out_col)

    def transposes_to(dst, src_bf, nchunks, R, dst_chunk0=0):
        c = 0
        while c < nchunks:
            n = min(8, nchunks - c)
            pst = psum()
            psv = pst[:, :].bitcast(BF16).rearrange("p (a b) -> p a b", a=8)
            for j in range(n):
                tr(psv[:, j, 0:R], src_bf[0:R, (c + j) * 128:(c + j + 1) * 128], identb[0:R, 0:R])
            cp(dst[:, dst_chunk0 + c:dst_chunk0 + c + n, 0:R], psv[:, 0:n, 0:R], eng="act")
            c += n

    def tr32_multi(pairs):
        for b0 in range(0, len(pairs), 4):
            grp = pairs[b0:b0 + 4]
            pst = psum()
            v = pst[:, :].bitcast(BF16).rearrange("p (a b) -> p a b", a=8)
            for j, (src, dst) in enumerate(grp):
                pr, w = src.shape
                hi = abf[0:pr, j * 128:j * 128 + w]
                lo = abf[0:pr, 1024 + j * 128:1024 + j * 128 + w]
                tmp = yacc[0:pr, j * 128:j * 128 + w]
                cp(hi, src)
                tt(tmp, src, hi, ALU.subtract)
                cp(lo, tmp)
                tr(v[0:w, j, 0:pr], hi, identb[0:pr, 0:pr])
                tr(v[0:w, 4 + j, 0:pr], lo, identb[0:pr, 0:pr])
            for j, (src, dst) in enumerate(grp):
                pr, w = src.shape
                cp(dst, v[0:w, j, 0:pr])
                tt(dst, dst, v[0:w, 4 + j, 0:pr], ALU.add)

    def rms_to_xT(R, gain, dst=None):
        act(abf[0:R], hb[0:R], AF.Square, accum=col(0)[0:R])
        rstd_of(col(0)[0:R], col(1)[0:R], D)
        stt(abf[0:R], hb[0:R], col(1)[0:R], gain[0:R], ALU.mult, ALU.mult)
        transposes_to(xT if dst is None else dst, abf, 16, R)

    def post_norm(R, gain):
        act(abf[0:R], yacc[0:R], AF.Square, accum=col(2)[0:R])
        rstd_of(col(2)[0:R], col(3)[0:R], D)
        stt(yacc[0:R], yacc[0:R], col(3)[0:R], gain[0:R], ALU.mult, ALU.mult)
        tt(hb[0:R], hb[0:R], yacc[0:R], ALU.add)

    def linear_tok(W, li, ncols, R, src, evac, K=D):
        nk = K // 128
        for g in range(ncols // 512):
            ps = psum()
            for kb in range(0, nk, 8):
                wt = wtile(W[li], kb * 128, g * 512, 8, 512)
                for k in range(8):
                    mm(ps[0:R, :], src[:, kb + k, 0:R], wt[:, k, :], start=(kb + k == 0), stop=(kb + k == nk - 1))
                pump(2)
            evac(g, ps)

    def ln_rows(R, x, n, gb, bb, out, c0):
        act(junk[0:R, 0:n], x, AF.Identity, accum=col(c0)[0:R])
        act(junk[0:R, 0:n], x, AF.Square, accum=col(c0 + 1)[0:R])
        ts(col(c0)[0:R], col(c0)[0:R], 1.0 / n, None, ALU.mult)
        tt(col(c0 + 2)[0:R], col(c0)[0:R], col(c0)[0:R], ALU.mult)
        stt(col(c0 + 1)[0:R], col(c0 + 1)[0:R], 1.0 / n, col(c0 + 2)[0:R], ALU.mult, ALU.subtract)
        rstd_of(col(c0 + 1)[0:R], col(c0 + 1)[0:R], 1.0)
        ts(x, x, col(c0)[0:R], col(c0 + 1)[0:R], ALU.subtract, ALU.mult)
        dma(lnt, gb.to_broadcast([128, 1024]))
        tt(x, x, lnt[0:R], ALU.mult)
        dma(lnt, bb.to_broadcast([128, 1024]))
        tt(out, x, lnt[0:R], ALU.add)

    pending = []
    fgrp = []

    def pump(n=2):
        for _ in range(n):
            while pending:
                try:
                    next(pending[0])
                    break
                except StopIteration:
                    pending.pop(0)
            if not pending:
                return

    def drain():
        while pending:
            pump(1000)

    def ffn_gen(l, grp):
        for part in range(8):
            for g in range(4):
                wt = wtile(w_up[l], 0, (part * 4 + g) * 256, 16, 256)
                if len(grp) == 2:
                    ps = gpsum()
                    for c in range(2):
                        for k in range(16):
                            mm(ps[:, c * 256:(c + 1) * 256], wt[:, k, c * 128:(c + 1) * 128], xTfa[:, k, :], start=(k == 0), stop=(k == 15))
                    pv = ps[:, :].rearrange("p (c r) -> p c r", c=2)
                    jv = junkF[:, :].rearrange("p (c r) -> p c r", c=2)
                    act(jv, pv, AF.Relu)
                    tt(hT[:, g * 2:(g + 1) * 2, :], jv, jv, ALU.mult)
                    yield
                    continue
                for j, (t, R, last) in enumerate(grp):
                    ps = gpsum()
                    for c in range(2):
                        for k in range(16):
                            mm(ps[:, c * 128:c * 128 + R], wt[:, k, c * 128:(c + 1) * 128], xTf[j][:, k, 0:R], start=(k == 0), stop=(k == 15))
                    pv = ps[:, 0:256].rearrange("p (c r) -> p c r", c=2)[:, :, 0:R]
                    jv = junkF[:, 0:256].rearrange("p (c r) -> p c r", c=2)[:, :, 0:R]
                    act(jv, pv, AF.Relu)
                    tt(hT[:, g * 2:(g + 1) * 2, j * 128:j * 128 + R], jv, jv, ALU.mult)
                yield
            for n in range(4):
                wt = wtile(w_down[l], part * 8 * 128, n * 512, 8, 512)
                for j, (t, R, last) in enumerate(grp):
                    ps = gpsum()
                    for k in range(8):
                        mm(ps[0:R, :], hT[:, k, j * 128:j * 128 + R], wt[:, k, :], start=(k == 0), stop=(k == 7))
                    ya = yaccF[j][0:R, n * 512:(n + 1) * 512]
                    if part == 0:
                        cp(ya, ps[0:R, :], eng="act")
                    else:
                        tt(ya, ya, ps[0:R, :], ALU.add)
                yield
        for j, (t, R, last) in enumerate(grp):
            jq = hT[0:R].rearrange("p a b -> p (a b)")
            ya = yaccF[j]
            act(jq, ya[0:R], AF.Square, accum=col(40)[0:R])
            rstd_of(col(40)[0:R], col(41)[0:R], D)
            gF = gain(3, l)
            stt(ya[0:R], ya[0:R], col(41)[0:R], gF[0:R], ALU.mult, ALU.mult)
            if last:
                dst = ys if t == NPT else yp[t * 128:(t + 1) * 128, :]
                key = "yout%d" % t
            else:
                dst = hscr[t, 0:R, :]
                key = "h%d" % t

            def acc_dma(e, dst=dst, src=ya[0:R]):
                return e.dma_start(out=dst, in_=src, accum_op=ALU.add)
            P.op("pool", acc_dma, reads=[ya[0:R]], writes=[key], dma=True)
            yield

    for l in range(DEPTH):
        i = l // 2
        even = (l % 2 == 0)
        curl[0] = l
        if even:
            dma(wpb, w_pool[i].rearrange("g c d -> c g d"), q="pool")
            dma(spc, s_pool[i].rearrange("(g c) -> c g", c=128))
            memset(pext, 0.0)
            memset(S, 0.0); memset(Sb, 0.0)
            dma(o_ps[i].rearrange("(s r) c -> s r c", r=15)[:, 0:11, :], spool[i].rearrange("(s r) c -> s r c", r=15)[:, 4:15, :],
                wk=["o_ps%d" % i])
        else:
            dma(junk[0:31, :], dw_w[i])
            tr32_multi([(junk[0:31, c * 128:(c + 1) * 128], dwT[:, c, :]) for c in range(8)])
            dma(dwbT, dw_b[i].rearrange("(c p) -> p c", p=128))
            dma(sgw, sg_w[i].rearrange("g a b -> a g b"))
            dma(sgbP, sg_b[i].rearrange("g a -> a g"))
            for s in range(16):
                dma(sgbS[4 * s:4 * s + 4, :], sg_b[i, :, 0:4].rearrange("g a -> a g"))
            cp(abf[:, 0:512].rearrange("p (g b) -> p g b", g=4), sgw)
            pst = psum()
            pv4 = pst[:, :].bitcast(BF16).rearrange("p (a b) -> p a b", a=8)
            for g in range(4):
                tr(pv4[:, g, :], abf[:, g * 128:(g + 1) * 128], identb)
            tt(wsT, pv4[:, 0:4, :], T["tri"].rearrange("p (o b) -> p o b", o=1).to_broadcast([128, 4, 128]), ALU.mult)
            for g in range(4):
                for s in range(16):
                    dma(junk[4 * s:4 * s + 4, 0:4], sg_w[i, g, 0:4, 0:4].rearrange("a b -> b a"))
                memset(junk[0:64, 64:128], 0.0)
                for s in range(16):
                    cp(junk[0:64, 64 + 4 * s:64 + 4 * s + 4], junk[0:64, 0:4])
                tt(wsTs[0:64, g, :], junk[0:64, 64:128], T["bd"][0:64], ALU.mult)
            memset(gext, 0.0)
            dma(o_cs[i].rearrange("(s r) c -> s r c", r=30)[:, 0:26, :], sconv[i].rearrange("(s r) c -> s r c", r=30)[:, 4:30, :],
                wk=["o_cs%d" % i])

        for t in range(NPT + 1):
            samp = (t == NPT)
            R = NS if samp else 128
            if l == 0:
                src = xs if samp else xp[t * 128:(t + 1) * 128, :]
                dma(hb[0:R], src)
            else:
                dma(hb[0:R], hscr[t, 0:R, :], rk=["h%d" % t])
            stage("load")
            rms_to_xT(R, gain(0, l))
            stage("rms0")

            if even:
                def evac_even(g, ps):
                    if g == 0:
                        cp(zbuf[0:R, 0:512], ps[0:R, :], eng=os.environ.get("KEV", "act"))
                    elif g <= 3:
                        x = ps[0:R, :].rearrange("p (h d) -> p h d", h=4)
                        cb = cosv[0:R, t:t + 1, :].to_broadcast([R, 4, 64])
                        sb_ = sinv[0:R, t:t + 1, :].to_broadcast([R, 4, 64])
                        o = qk[0:R, (g - 1) * 512:g * 512].rearrange("p (h d) -> p h d", h=4)
                        tt(rt[0][0:R, 0:4], x[:, :, 0:64], cb, ALU.mult)
                        tt(rt[1][0:R, 0:4], x[:, :, 64:128], sb_, ALU.mult)
                        tt(o[:, :, 0:64], rt[0][0:R, 0:4], rt[1][0:R, 0:4], ALU.subtract)
                        tt(rt[2][0:R, 0:4], x[:, :, 0:64], sb_, ALU.mult)
                        tt(rt[3][0:R, 0:4], x[:, :, 64:128], cb, ALU.mult)
                        tt(o[:, :, 64:128], rt[2][0:R, 0:4], rt[3][0:R, 0:4], ALU.add)
                    elif g <= 6:
                        cp(vt[0:R, (g - 4) * 512:(g - 3) * 512], ps[0:R, :], eng="act")
                    else:
                        act(zbuf[0:R, 512 + (g - 7) * 512:512 + (g - 6) * 512], ps[0:R, :], AF.Silu)
                linear_tok(w_in_even, i, 5120, R, xT, evac_even)
                stage("win")
                gs = zbuf[:, 512:2048]
                if samp:
                    for hf in range(2):
                        dma(hloadE[0:120, 0:512], spool[i, hf * 120:(hf + 1) * 120, :])
                        prs = []
                        for c in range(4):
                            prs.append((hloadE[0:120, c * 128:(c + 1) * 128], hstg[:, c, 0:120]))
                        tr32_multi(prs)
                        for c in range(4):
                            cp(pextS[:, c, :].rearrange("p (s r) -> p s r", r=19)[:, hf * 8:(hf + 1) * 8, 0:15],
                               hstg[:, c, 0:120].rearrange("p (s r) -> p s r", r=15))
                if samp:
                    for s in range(16):
                        dma(o_ps[i, s * 15 + 11:s * 15 + 15, :], zbuf[4 * s:4 * s + 4, 0:512], wk=["o_ps%d" % i])
                if samp:
                    tr32_multi([(zbuf[0:R, c * 128:(c + 1) * 128], hstg[:, c, 0:R]) for c in range(4)])
                else:
                    tr32_multi([(zbuf[0:R, c * 128:(c + 1) * 128], pext[:, c, 16:144]) for c in range(4)])
                for c in range(4):
                    w = POOLW[c]
                    if samp:
                        E = pextS[:, c, :]
                        cp(E.rearrange("p (s r) -> p s r", r=19)[:, :, 15:19], hstg[:, c, 0:64].rearrange("p (s r) -> p s r", r=4))
                        L = 304
                    else:
                        E = pext[:, c, :]
                        L = 144
                    a_, b_ = E, pA
                    sh = 1
                    while sh < w:
                        tt(b_[:, sh:L], a_[:, sh:L], a_[:, 0:L - sh], ALU.add)
                        a_ = b_
                        b_ = pB if b_ is pA else pA
                        sh *= 2
                    Ssum = a_
                    if samp:
                        stt(dT[:, c, 0:64].rearrange("p (s r) -> p s r", r=4),
                            Ssum[:, 0:304].rearrange("p (s r) -> p s r", r=19)[:, :, 15:19], 1.0 / w,
                            E.rearrange("p (s r) -> p s r", r=19)[:, :, 15:19], ALU.mult, ALU.subtract)
                    else:
                        stt(dT[:, c, :], Ssum[:, 16:144], 1.0 / w, E[:, 16:144], ALU.mult, ALU.subtract)
                        if t == 0:
                            tt(rt[0][:, 0, 0:16], Ssum[:, 16:32], T["invc"][:, c, :], ALU.mult)
                            tt(dT[:, c, 0:16], rt[0][:, 0, 0:16], E[:, 16:32], ALU.subtract)
                    ps2 = psum()
                    mm(ps2[:, 0:R], wpb[:, c, :], dT[:, c, 0:R])
                    act(yT[:, c, 0:R], ps2[:, 0:R], AF.Copy, scale=spc[:, c:c + 1])
                    if not samp:
                        cp(E[:, 1:16], E[:, 129:144])
                    pump(2)
                if t == NPT - 1:
                    dma(o_pp[i], zbuf[113:128, 0:512], wk=["o_pp%d" % i])
                stage("pool")
                psq = psum(); psk = psum()
                qv = psq[:, :].bitcast(BF16).rearrange("p (a b) -> p a b", a=8)
                kv = psk[:, :].bitcast(BF16).rearrange("p (a b) -> p a b", a=8)
                for h in range(HEADS):
                    tr(qv[:, h, 0:R], qk[0:R, h * 128:(h + 1) * 128], identb[0:R, 0:R])
                    tr(kv[:, h, 0:R], qk[0:R, 768 + h * 128:768 + (h + 1) * 128], identb[0:R, 0:R])
                cp(qT[:, :, 0:R], qv[:, 0:HEADS, 0:R], eng="act")
                tt(qdT[:, :, 0:R], qv[:, 0:HEADS, 0:R], (T["qdecS"] if samp else T["qdecT"])[:, :, 0:R], ALU.mult)
                cp(kT[:, :, 0:R], kv[:, 0:HEADS, 0:R], eng="act")
                kdt = T["kdecS"] if samp else T["kdec"]
                tt(kd[0:R], qk[0:R, 768:1536].rearrange("p (h d) -> p h d", h=HEADS),
                   kdt[0:R].rearrange("p (h o) -> p h o", o=1).to_broadcast([R, HEADS, 128]), ALU.mult)
                psc = [psum(), psum()]
                for h in range(HEADS):
                    mm(psc[h // 4][0:R, (h % 4) * 128:(h % 4) * 128 + R], kT[:, h, 0:R], qT[:, h, 0:R])
                mk = maskSb if samp else maskTb
                tt(PT[0:R, 0:4, 0:R], psc[0][0:R, :].rearrange("p (h l) -> p h l", h=4)[:, :, 0:R], mk[0:R, 0:4, 0:R], ALU.mult)
                tt(PT[0:R, 4:6, 0:R], psc[1][0:R, 0:256].rearrange("p (h l) -> p h l", h=2)[:, :, 0:R], mk[0:R, 4:6, 0:R], ALU.mult)
                if not samp:
                    pso = [psum(), psum(), psum()]

                    def oap(h):
                        return pso[h // 2][0:R, (h % 2) * 256:(h % 2 + 1) * 256]
                    for h in range(HEADS):
                        mm(oap(h), PT[0:R, h, 0:R], vt[0:R, h * 256:(h + 1) * 256], start=True, stop=False)
                        mm(oap(h), qdT[:, h, 0:R], Sb[:, h, :], start=False, stop=True)
                    pss = [psum(), psum(), psum()]
                    for h in range(HEADS):
                        sp_ = pss[h // 2][:, (h % 2) * 256:(h % 2 + 1) * 256]
                        mm(sp_, kd[0:R, h, :], vt[0:R, h * 256:(h + 1) * 256])
                        stt(S[:, h, :], S[:, h, :], float(cd[h]), sp_, ALU.mult, ALU.add)
                        cp(Sb[:, h, :], S[:, h, :], eng="act")
                    if t == NPT - 1:
                        dma(o_rp[i].rearrange("h d e -> d h e"), S, wk=["o_rp%d" % i])
                else:
                    oacc = yacc[0:64, 0:1536]
                    pso = [psum(), psum(), psum()]
                    for h in range(HEADS):
                        oh = pso[h // 2][0:R, (h % 2) * 256:(h % 2 + 1) * 256]
                        mm(oh, PT[0:R, h, 0:R], vt[0:R, h * 256:(h + 1) * 256])
                    for hp in range(3):
                        cp(oacc[:, hp * 512:(hp + 1) * 512], pso[hp][0:R, :], eng="act")
                    for s in range(16):
                        dma(S0, sret[i, s].rearrange("h d e -> d h e"))
                        cp(S0b, S0, eng="act")
                        tt(qdm, qdT[:, :, 0:64], T["colmask"][:, s:s + 1, :].to_broadcast([128, HEADS, 64]), ALU.mult)
                        ts(kdm[0:64], kd[0:64].rearrange("p h d -> p (h d)"), T["onehot"][0:64, s:s + 1], None, ALU.mult)
                        pso = [psum(), psum(), psum()]
                        for h in range(HEADS):
                            oh = pso[h // 2][0:R, (h % 2) * 256:(h % 2 + 1) * 256]
                            mm(oh, qdm[:, h, :], S0b[:, h, :])
                        for hp in range(3):
                            tt(oacc[:, hp * 512:(hp + 1) * 512], oacc[:, hp * 512:(hp + 1) * 512], pso[hp][0:R, :], ALU.add)
                        pss = [psum(), psum(), psum()]
                        for h in range(HEADS):
                            sp_ = pss[h // 2][:, (h % 2) * 256:(h % 2 + 1) * 256]
                            mm(sp_, kdm[0:64, h * 128:(h + 1) * 128], vt[0:64, h * 256:(h + 1) * 256])
                            stt(S0[:, h, :], S0[:, h, :], float(cd4[h]), sp_, ALU.mult, ALU.add)
                        dma(o_rs[i, s].rearrange("h d e -> d h e"), S0, wk=["o_rs%d_%d" % (i, s)])
                        pump(3)

                    def oap(h):
                        return oacc[:, h * 256:(h + 1) * 256]
                stage("ret")
                for h in range(HEADS):
                    act(junk[0:R, 0:256], oap(h), AF.Square, accum=col(8 + h)[0:R])
                act(small[0:R, 16:22], small[0:R, 8:14], AF.Sqrt, bias=EPS, scale=1.0 / 256)
                recip(small[0:R, 16:22], small[0:R, 16:22])
                for h in range(HEADS):
                    stt(yin[0:R, h * 256:(h + 1) * 256], oap(h), col(16 + h)[0:R], gs[0:R, h * 256:(h + 1) * 256], ALU.mult, ALU.mult)
                transposes_to(yT, yin, 12, R, dst_chunk0=4)
                Wout = w_out_even
            else:
                def evac_odd(g, ps):
                    if g < 4:
                        act(zbuf[0:R, g * 512:(g + 1) * 512], ps[0:R, :], AF.Gelu_apprx_tanh)
                    elif g < 6:
                        cp(zbuf[0:R, g * 512:(g + 1) * 512], ps[0:R, :], eng="act")
                    else:
                        act(junk[0:R, 0:512], ps[0:R, :], AF.Sigmoid)
                        tt(zbuf[0:R, (g - 2) * 512:(g - 1) * 512], zbuf[0:R, (g - 2) * 512:(g - 1) * 512], junk[0:R, 0:512], ALU.mult)
                linear_tok(w_in_odd, i, 4096, R, xT, evac_odd)
                ug = zbuf[:, 0:1024]; vv = zbuf[:, 1024:2048]; glu = zbuf[:, 2048:3072]
                ln_rows(R, vv[0:R], 1024, sg_ln_g[i:i + 1, :], sg_ln_b[i:i + 1, :], vv[0:R], 24)
                if samp:
                    dma(o_sv[i], vv[0:64], wk=["o_sv%d" % i])
                    for s in range(16):
                        dma(o_cs[i, s * 30 + 26:s * 30 + 30, :], glu[4 * s:4 * s + 4], wk=["o_cs%d" % i])
                cp(abf[0:R, 0:1024], vv[0:R], eng="act")
                for g in range(4):
                    ps = psum()
                    if samp:
                        mm(ps[0:64, 0:256], wsTs[0:64, g, :], abf[0:64, g * 256:(g + 1) * 256])
                        bcol = sgbS[0:64, g:g + 1]
                    else:
                        mm(ps[:, 0:256], wsT[:, g, :], abf[:, g * 256:(g + 1) * 256])
                        bcol = sgbP[:, g:g + 1]
                    stt(yin[0:R, g * 256:(g + 1) * 256], ps[0:R, 0:256], bcol, ug[0:R, g * 256:(g + 1) * 256], ALU.add, ALU.mult)
                if samp:
                    for q4 in range(4):
                        dma(hloadO[0:120, :], sconv[i, q4 * 120:(q4 + 1) * 120, :])
                        tr32_multi([(hloadO[0:120, c * 128:(c + 1) * 128], hstg[:, c, 0:120]) for c in range(8)])
                        for c in range(8):
                            cp(gextS[:, c, :].rearrange("p (s r) -> p s r", r=34)[:, q4 * 4:(q4 + 1) * 4, 0:30],
                               hstg[:, c, 0:120].rearrange("p (s r) -> p s r", r=30))
                if samp:
                    tr32_multi([(glu[0:R, c * 128:(c + 1) * 128], hstg[:, c, 0:R]) for c in range(8)])
                else:
                    tr32_multi([(glu[0:R, c * 128:(c + 1) * 128], gext[:, c, 32:160]) for c in range(8)])
                for c in range(8):
                    if samp:
                        E3 = gextS[:, c, :].rearrange("p (s r) -> p s r", r=34)
                        cp(E3[:, :, 30:34], hstg[:, c, 0:64].rearrange("p (s r) -> p s r", r=4))
                        acc = cacc[:, c, 0:64].rearrange("p (s r) -> p s r", r=4)
                        ts(acc, E3[:, :, 0:4], dwT[:, c, 0:1], dwbT[:, c:c + 1], ALU.mult, ALU.add)
                        for j in range(1, 31):
                            stt(acc, E3[:, :, j:j + 4], dwT[:, c, j:j + 1], acc, ALU.mult, ALU.add)
                    else:
                        E = gext[:, c, :]
                        acc = cacc[:, c, :]
                        ts(acc, E[:, 2:130], dwT[:, c, 0:1], dwbT[:, c:c + 1], ALU.mult, ALU.add)
                        for j in range(1, 31):
                            stt(acc, E[:, 2 + j:130 + j], dwT[:, c, j:j + 1], acc, ALU.mult, ALU.add)
                        cp(E[:, 2:32], E[:, 130:160])
                    pump(3)
                if t == NPT - 1:
                    dma(o_cp[i], glu[98:128], wk=["o_cp%d" % i])
                tr32_multi([(cacc[:, c, 0:R], zbuf[0:R, 3072 + c * 128:3072 + (c + 1) * 128]) for c in range(8)])
                cv = zbuf[:, 3072:4096]
                ln_rows(R, cv[0:R], 1024, cv_ln_g[i:i + 1, :], cv_ln_b[i:i + 1, :], cv[0:R], 28)
                act(yin[0:R, 1024:2048], cv[0:R], AF.Silu)
                transposes_to(yT, yin, 16, R)
                Wout = w_out_odd

            stage("mix")
            def evac_y(g, ps):
                cp(yacc[0:R, g * 512:(g + 1) * 512], ps[0:R, :], eng="act")
            linear_tok(Wout, i, D, R, yT, evac_y)
            drain()
            post_norm(R, gain(1, l))
            stage("wout")
            last = (l == DEPTH - 1)
            if last:
                dma(ys if samp else yp[t * 128:(t + 1) * 128, :], hb[0:R], wk=["yout%d" % t])
            else:
                dma(hscr[t, 0:R, :], hb[0:R], wk=["h%d" % t])
            rms_to_xT(R, gain(2, l), dst=xTf[len(fgrp)])
            fgrp.append((t, R, last))
            if len(fgrp) == 2 or samp:
                pending.append(ffn_gen(l, list(fgrp)))
                del fgrp[:]
            stage("ffn")
    drain()

    for q in ("sp", "pool"):
        n = P.ndma[q]
        toks = [("d", q, k) for k in range(max(0, n - RING[q]), n)]
        waits = [tk for tk in toks if tk not in P.dknown["sp"]]
        P.ops["sp"].append((waits, None, ("c", "sp", len(P.ops["sp"]))))

    with nc.Block() as block:
        P.emit(nc, block, sems, dsems)
    es.close()
    return nc


_CACHE = {}


def kernel(**inp):
    tabs, cd, cd4 = host_tables()
    if "nc" not in _CACHE:
        _CACHE["nc"] = build(cd, cd4)
    nc = _CACHE["nc"]
    f = lambda a: np.ascontiguousarray(np.asarray(a, dtype=np.float32))
    norms = f(np.concatenate([inp["norm_mix_pre"], inp["norm_mix_post"], inp["norm_ffn_pre"], inp["norm_ffn_post"]], 0))
    shared = {k: f(inp[k]) for k in ("w_in_even", "w_pool", "s_pool", "w_out_even", "w_in_odd", "sg_ln_g", "sg_ln_b", "sg_w",
                                     "sg_b", "dw_w", "dw_b", "cv_ln_g", "cv_ln_b", "w_out_odd", "w_up", "w_down")}
    shared["norms"] = norms
    for k, v in tabs.items():
        shared["t_" + k] = f(v)
    xpr = f(inp["x_prompt"]); xsm = f(inp["x_sample"])
    sp_ = f(inp["state_pool"]); sr_ = f(inp["state_ret"]); sc_ = f(inp["state_conv"])
    in_maps = []
    for c in range(8):
        m = dict(shared)
        m["xp"] = xpr[c % 4]
        m["xs"] = xsm[16 * c:16 * c + 16].reshape(NS, D)
        m["spool"] = np.ascontiguousarray(sp_[:, 16 * c:16 * c + 16].reshape(2, 240, 512))
        m["sret"] = np.ascontiguousarray(sr_[:, 16 * c:16 * c + 16])
        m["sconv"] = np.ascontiguousarray(sc_[:, 16 * c:16 * c + 16].reshape(2, 480, 1024))
        in_maps.append(m)
    res = run_bass_kernel_spmd(nc, in_maps, core_ids=list(range(8)))
    R_ = res.results
    y_prompt = np.stack([R_[b]["yp"] for b in range(4)], 0)
    y_sample = np.concatenate([R_[c]["ys"].reshape(16, 4, D) for c in range(8)], 0)
    pool_p = np.stack([R_[b]["o_pp"] for b in range(4)], 1)
    pool_s = np.concatenate([R_[c]["o_ps"].reshape(2, 16, 15, 512) for c in range(8)], 1)
    ret_p = np.stack([R_[b]["o_rp"] for b in range(4)], 1)
    ret_s = np.concatenate([R_[c]["o_rs"] for c in range(8)], 1)
    conv_p = np.stack([R_[b]["o_cp"] for b in range(4)], 1)
    conv_s = np.concatenate([R_[c]["o_cs"].reshape(2, 16, 30, 1024) for c in range(8)], 1)
    sgv_s = np.concatenate([R_[c]["o_sv"].reshape(2, 16, 4, 1024) for c in range(8)], 1)
    outs = (y_prompt, y_sample, pool_p, pool_s, ret_p, ret_s, conv_p, conv_s, sgv_s)
    return tuple(np.ascontiguousarray(o, dtype=np.float32) for o in outs)
```
